# Optimizing a Trainium2 kernel written in Bass

```python
import jax, jax.numpy as jnp
from jax import lax
import numpy as np

D_MODEL = 1024
BATCH = 8
SEQ = 2048
DEPTH = 2
DEC_BATCH = 128
DEC_SEQ = 4
PAST_LEN = 16384
PAGE_SIZE = 128

D_LRU = D_MODEL // 2
LRU_BLOCKS = 8
LRU_BW = D_LRU // LRU_BLOCKS
CONV_W = 4
LRU_C = 8.0
D_RWKV = D_MODEL - D_LRU
RWKV_HEAD = 64
RWKV_HEADS = D_RWKV // RWKV_HEAD
R_DECAY = 64
R_ICLR = 64
R_GATE = 128
D_MIX = D_LRU + D_RWKV
RWKV_PROJ_W = 3 * D_RWKV + R_DECAY + R_ICLR + R_GATE
IN_W = 2 * D_LRU + RWKV_PROJ_W
D_FF = 256 * ((8 * D_MODEL // 3 + 255) // 256)
N_MOD = 9
NORM_EPS = 1e-6
GN_EPS = 64e-5
L2_EPS = 1e-12

kernel_name = "hybrid_rglru_rwkv7_adaln_step"


def _rms(x, g):
    xf = x.astype(jnp.float32)
    y = xf * lax.rsqrt(jnp.mean(xf * xf, axis=-1, keepdims=True) + NORM_EPS)
    return (y * g.astype(jnp.float32)).astype(x.dtype)


def _modulate(h, shift, scale):
    return h * (1 + scale[:, None, :]) + shift[:, None, :]


def _swiglu(h, w_gate, w_up, w_down):
    return (jax.nn.silu(h @ w_gate) * (h @ w_up)) @ w_down


def _linear_scan(a, b, h0):
    b = b.at[:, 0].add(a[:, 0] * h0)

    def combine(left, right):
        a_l, b_l = left
        a_r, b_r = right
        return a_l * a_r, a_r * b_l + b_r

    _, h = lax.associative_scan(combine, (a, b), axis=1)
    return h


def _wkv7(S0, r, w, k, v, a, b):
    def step(S, inp):
        r_t, w_t, k_t, v_t, a_t, b_t = inp
        sa = jnp.einsum("bhvk,bhk->bhv", S, a_t)
        S = S * w_t[:, :, None, :] + sa[..., None] * b_t[:, :, None, :] + v_t[..., None] * k_t[:, :, None, :]
        return S, jnp.einsum("bhvk,bhk->bhv", S, r_t)

    xs = tuple(jnp.moveaxis(t, 1, 0) for t in (r, w, k, v, a, b))
    S, ys = lax.scan(step, S0, xs)
    return jnp.moveaxis(ys, 0, 1), S


def _rglru_group(xb, gb, conv_buf, h0, p):
    B, T, _ = xb.shape
    f32 = jnp.float32
    xpad = jnp.concatenate([conv_buf.astype(xb.dtype), xb], axis=1)
    cw = p["lru_conv_w"]
    xc = xpad[:, CONV_W - 1:] * cw[CONV_W - 1] + p["lru_conv_b"]
    for j in range(CONV_W - 1):
        xc = xc + xpad[:, j:j + T] * cw[j]
    xcb = xc.reshape(B, T, LRU_BLOCKS, LRU_BW)
    gate_x = jax.nn.sigmoid((jnp.einsum("btnc,ncd->btnd", xcb, p["lru_wx"]).reshape(B, T, D_LRU) + p["lru_bx"]).astype(f32))
    gate_a = jax.nn.sigmoid((jnp.einsum("btnc,ncd->btnd", xcb, p["lru_wa"]).reshape(B, T, D_LRU) + p["lru_ba"]).astype(f32))
    log_a = -LRU_C * gate_a * jax.nn.softplus(-p["lru_lambda"].astype(f32))
    a = jnp.exp(log_a)
    b = jnp.sqrt(-jnp.expm1(2.0 * log_a)) * gate_x * xc.astype(f32)
    h = _linear_scan(a, b, h0.astype(f32))
    y = (h * jax.nn.gelu(gb.astype(f32))).astype(xb.dtype)
    return y, xpad[:, T:], h[:, -1]


def _rwkv7_group(pr, shift_buf, S0, p):
    B, T, _ = pr.shape
    f32 = jnp.float32
    prev = jnp.concatenate([shift_buf[:, None, :].astype(pr.dtype), pr[:, :-1]], axis=1)
    xs = pr + (prev - pr) * p["rwkv_mu"]
    o1, o2, o3 = D_RWKV, 2 * D_RWKV, 3 * D_RWKV
    o4, o5 = o3 + R_DECAY, o3 + R_DECAY + R_ICLR
    r, k, v = xs[..., :o1], xs[..., o1:o2], xs[..., o2:o3]
    wd, ad, gd = xs[..., o3:o4], xs[..., o4:o5], xs[..., o5:]
    w_log = -jax.nn.softplus(-(p["rwkv_w0"] + jnp.tanh(wd) @ p["rwkv_w2"]).astype(f32)) - 0.5
    decay = jnp.exp(-jnp.exp(w_log))
    a_rate = jax.nn.sigmoid((p["rwkv_a0"] + ad @ p["rwkv_a2"]).astype(f32))
    g = (jax.nn.sigmoid(gd) @ p["rwkv_g2"]).astype(f32)

    def heads(t):
        return t.reshape(B, T, RWKV_HEADS, RWKV_HEAD)

    kf = k.astype(f32)
    kk = heads(kf * p["rwkv_k_k"])
    kk = kk / jnp.maximum(jnp.sqrt(jnp.sum(kk * kk, axis=-1, keepdims=True)), L2_EPS)
    kf = kf * (1.0 + (a_rate - 1.0) * p["rwkv_k_a"])
    rh, kh, vh = heads(r.astype(f32)), heads(kf), heads(v.astype(f32))
    y, S = _wkv7(S0.astype(f32), rh, heads(decay), kh, vh, -kk, kk * heads(a_rate))
    mu = jnp.mean(y, axis=-1, keepdims=True)
    var = jnp.mean(jnp.square(y - mu), axis=-1, keepdims=True)
    yn = ((y - mu) * lax.rsqrt(var + GN_EPS)).reshape(B, T, D_RWKV) * p["rwkv_ln_w"] + p["rwkv_ln_b"]
    bonus = jnp.sum(rh * kh * p["rwkv_r_k"], axis=-1, keepdims=True) * vh
    out = (yn + bonus.reshape(B, T, D_RWKV)) * g
    return out.astype(pr.dtype), pr[:, -1], S


def _mixer(h, conv_buf, h0, shift_buf, S0, p):
    proj = h @ p["w_in"]
    y_lru, n_conv, n_h = _rglru_group(proj[..., :D_LRU], proj[..., D_LRU:2 * D_LRU], conv_buf, h0, p)
    y_rw, n_shift, n_S = _rwkv7_group(proj[..., 2 * D_LRU:], shift_buf, S0, p)
    out = jnp.concatenate([y_lru, y_rw], axis=-1) @ p["w_out"]
    return out, (n_conv, n_h, n_shift, n_S)


def _trunk(x, c, conv, hst, shift, wkv, params, final_norm):
    cs = jax.nn.silu(c)
    outs = ([], [], [], [])
    for l in range(DEPTH):
        p = {name: arr[l] for name, arr in params.items()}
        mod = cs @ p["w_ada"] + p["b_ada"]
        sh1, sc1, g1, sh2, sc2, g2, sh3, sc3, g3 = jnp.split(mod, N_MOD, axis=-1)
        h = _modulate(_rms(x, p["ffn1_norm"]), sh1, sc1)
        x = x + 0.5 * g1[:, None, :] * _swiglu(h, p["ffn1_w_gate"], p["ffn1_w_up"], p["ffn1_w_down"])
        h = _modulate(_rms(x, p["mix_norm"]), sh2, sc2)
        m, new = _mixer(h, conv[l], hst[l], shift[l], wkv[l], p)
        x = x + g2[:, None, :] * m
        h = _modulate(_rms(x, p["ffn2_norm"]), sh3, sc3)
        x = x + 0.5 * g3[:, None, :] * _swiglu(h, p["ffn2_w_gate"], p["ffn2_w_up"], p["ffn2_w_down"])
        for lst, s in zip(outs, new):
            lst.append(s.astype(x.dtype))
    y = _rms(x, final_norm)
    return y, tuple(jnp.stack(lst) for lst in outs)


def setup_inputs(seed: int = 0) -> dict:
    key = jax.random.key(seed)
    ks = list(jax.random.split(key, 48))
    f32 = jnp.float32

    def nrm(i, shape, s):
        return s * jax.random.normal(ks[i], shape, f32)

    d = D_MODEL
    lam_u = jax.random.uniform(ks[40], (DEPTH, D_LRU), f32, 0.9, 0.999)
    lam_s = lam_u ** (1.0 / LRU_C)
    return {
        "x_prompt": nrm(0, (BATCH, SEQ, d), 1.0),
        "x_sample": nrm(1, (DEC_BATCH, DEC_SEQ, d), 1.0),
        "c_prompt": nrm(2, (BATCH, d), 1.0),
        "c_sample": nrm(3, (DEC_BATCH, d), 1.0),
        "state_lru_conv": nrm(4, (DEPTH, DEC_BATCH, CONV_W - 1, D_LRU), 1.0),
        "state_lru_h": nrm(5, (DEPTH, DEC_BATCH, D_LRU), 0.5),
        "state_rwkv_shift": nrm(6, (DEPTH, DEC_BATCH, RWKV_PROJ_W), 1.0),
        "state_rwkv_wkv": nrm(7, (DEPTH, DEC_BATCH, RWKV_HEADS, RWKV_HEAD, RWKV_HEAD), 0.3),
        "w_ada": nrm(8, (DEPTH, d, N_MOD * d), 0.5 * d ** -0.5),
        "b_ada": nrm(9, (DEPTH, N_MOD * d), 0.02),
        "ffn1_norm": 1.0 + nrm(10, (DEPTH, d), 0.05),
        "ffn1_w_gate": nrm(11, (DEPTH, d, D_FF), d ** -0.5),
        "ffn1_w_up": nrm(12, (DEPTH, d, D_FF), d ** -0.5),
        "ffn1_w_down": nrm(13, (DEPTH, D_FF, d), D_FF ** -0.5),
        "mix_norm": 1.0 + nrm(14, (DEPTH, d), 0.05),
        "w_in": nrm(15, (DEPTH, d, IN_W), d ** -0.5),
        "w_out": nrm(16, (DEPTH, D_MIX, d), D_MIX ** -0.5),
        "lru_conv_w": nrm(17, (DEPTH, CONV_W, D_LRU), CONV_W ** -0.5),
        "lru_conv_b": nrm(18, (DEPTH, D_LRU), 0.02),
        "lru_wx": nrm(19, (DEPTH, LRU_BLOCKS, LRU_BW, LRU_BW), LRU_BW ** -0.5),
        "lru_bx": nrm(20, (DEPTH, D_LRU), 0.02),
        "lru_wa": nrm(21, (DEPTH, LRU_BLOCKS, LRU_BW, LRU_BW), LRU_BW ** -0.5),
        "lru_ba": nrm(22, (DEPTH, D_LRU), 0.02),
        "lru_lambda": jnp.log(lam_s) - jnp.log1p(-lam_s),
        "rwkv_mu": jax.random.uniform(ks[23], (DEPTH, RWKV_PROJ_W), f32, 0.0, 1.0),
        "rwkv_w0": jax.random.uniform(ks[24], (DEPTH, D_RWKV), f32, -6.0, -1.0),
        "rwkv_w2": nrm(25, (DEPTH, R_DECAY, D_RWKV), 0.5 * R_DECAY ** -0.5),
        "rwkv_a0": nrm(26, (DEPTH, D_RWKV), 0.5),
        "rwkv_a2": nrm(27, (DEPTH, R_ICLR, D_RWKV), 0.5 * R_ICLR ** -0.5),
        "rwkv_g2": nrm(28, (DEPTH, R_GATE, D_RWKV), R_GATE ** -0.5),
        "rwkv_k_k": 0.85 + nrm(29, (DEPTH, D_RWKV), 0.05),
        "rwkv_k_a": 1.0 + nrm(30, (DEPTH, D_RWKV), 0.05),
        "rwkv_r_k": nrm(31, (DEPTH, RWKV_HEADS, RWKV_HEAD), 0.1),
        "rwkv_ln_w": 1.0 + nrm(32, (DEPTH, D_RWKV), 0.05),
        "rwkv_ln_b": nrm(33, (DEPTH, D_RWKV), 0.02),
        "ffn2_norm": 1.0 + nrm(34, (DEPTH, d), 0.05),
        "ffn2_w_gate": nrm(35, (DEPTH, d, D_FF), d ** -0.5),
        "ffn2_w_up": nrm(36, (DEPTH, d, D_FF), d ** -0.5),
        "ffn2_w_down": nrm(37, (DEPTH, D_FF, d), D_FF ** -0.5),
        "final_norm": 1.0 + nrm(38, (d,), 0.05),
    }


def reference(x_prompt, x_sample, c_prompt, c_sample, state_lru_conv, state_lru_h, state_rwkv_shift, state_rwkv_wkv,
              w_ada, b_ada, ffn1_norm, ffn1_w_gate, ffn1_w_up, ffn1_w_down, mix_norm, w_in, w_out,
              lru_conv_w, lru_conv_b, lru_wx, lru_bx, lru_wa, lru_ba, lru_lambda,
              rwkv_mu, rwkv_w0, rwkv_w2, rwkv_a0, rwkv_a2, rwkv_g2, rwkv_k_k, rwkv_k_a, rwkv_r_k, rwkv_ln_w, rwkv_ln_b,
              ffn2_norm, ffn2_w_gate, ffn2_w_up, ffn2_w_down, final_norm):
    params = dict(
        w_ada=w_ada, b_ada=b_ada, ffn1_norm=ffn1_norm, ffn1_w_gate=ffn1_w_gate, ffn1_w_up=ffn1_w_up,
        ffn1_w_down=ffn1_w_down, mix_norm=mix_norm, w_in=w_in, w_out=w_out,
        lru_conv_w=lru_conv_w, lru_conv_b=lru_conv_b, lru_wx=lru_wx, lru_bx=lru_bx, lru_wa=lru_wa,
        lru_ba=lru_ba, lru_lambda=lru_lambda, rwkv_mu=rwkv_mu, rwkv_w0=rwkv_w0, rwkv_w2=rwkv_w2,
        rwkv_a0=rwkv_a0, rwkv_a2=rwkv_a2, rwkv_g2=rwkv_g2, rwkv_k_k=rwkv_k_k, rwkv_k_a=rwkv_k_a,
        rwkv_r_k=rwkv_r_k, rwkv_ln_w=rwkv_ln_w, rwkv_ln_b=rwkv_ln_b, ffn2_norm=ffn2_norm,
        ffn2_w_gate=ffn2_w_gate, ffn2_w_up=ffn2_w_up, ffn2_w_down=ffn2_w_down)
    B = x_prompt.shape[0]
    dt = x_prompt.dtype
    z_conv = jnp.zeros((DEPTH, B, CONV_W - 1, D_LRU), dt)
    z_h = jnp.zeros((DEPTH, B, D_LRU), dt)
    z_shift = jnp.zeros((DEPTH, B, RWKV_PROJ_W), dt)
    z_wkv = jnp.zeros((DEPTH, B, RWKV_HEADS, RWKV_HEAD, RWKV_HEAD), dt)
    y_prompt, p_state = _trunk(x_prompt, c_prompt, z_conv, z_h, z_shift, z_wkv, params, final_norm)
    y_sample, s_state = _trunk(x_sample, c_sample, state_lru_conv, state_lru_h, state_rwkv_shift, state_rwkv_wkv,
                               params, final_norm)
    p_conv, p_h, p_shift, p_wkv = p_state
    s_conv, s_h, s_shift, s_wkv = s_state
    return (y_prompt, y_sample, p_conv, p_h, p_shift, p_wkv, s_conv, s_h, s_shift, s_wkv)
```

```python
import os
import types
import numpy as np
from contextlib import ExitStack
import concourse.bass as bass
import concourse.mybir as mybir
from concourse.bass_utils import run_bass_kernel_spmd

F32 = mybir.dt.float32
BF16 = mybir.dt.bfloat16
AF = mybir.ActivationFunctionType
ALU = mybir.AluOpType
AX = mybir.AxisListType

L = 2
D = 1024
NP_ = 2048
NB = 16
NS = 64
NT = NP_ + NS
NG = 17
DFF = 2816
TT = [(0, 512), (512, 512), (1024, 512), (1536, 512), (2048, 64)]
CH = 128
NORM_EPS = 1e-6
GN_EPS = 64e-5


def _freeze(fn):
    if fn.__closure__ is None:
        return fn
    cells = []
    for c in fn.__closure__:
        try:
            cells.append(types.CellType(c.cell_contents))
        except ValueError:
            cells.append(c)
    return types.FunctionType(fn.__code__, fn.__globals__, fn.__name__, fn.__defaults__, tuple(cells))


class Buf:
    __slots__ = ("name", "w", "r", "excl")

    def __init__(self, name, excl=False):
        self.name = name
        self.excl = excl
        self.w = None
        self.r = []


class Sched:
    ENG = ("pe", "act", "dve", "pool", "sp")

    def __init__(self, nc, stack, same_engine_sync=("pool",)):
        self.nc = nc
        self.stack = stack
        self.root = stack
        self.sems = {}
        self.cnt = {}
        for e in self.ENG:
            self.sems[e] = stack.enter_context(nc.semaphore("s_" + e))
            self.cnt[e] = 0
        self.known = {e: {} for e in self.ENG}
        self.prog = {e: [] for e in self.ENG}
        self.same = set(same_engine_sync)
        self.dma_sems = {}
        self.dma_cnt = {}
        self.nbuf = 0
        self.nt = 0

    def sb(self, shape, dt=F32, name=None):
        self.nt += 1
        return self.stack.enter_context(self.nc.sbuf_tensor(name or f"t{self.nt}", list(shape), dt))

    def ps(self, shape, dt=F32, name=None):
        self.nt += 1
        return self.stack.enter_context(self.nc.psum_tensor(name or f"p{self.nt}", list(shape), dt))

    def buf(self, name=None, excl=False):
        self.nbuf += 1
        return Buf(name or f"b{self.nbuf}", excl)

    def dma_sem(self, key):
        if key not in self.dma_sems:
            self.dma_sems[key] = self.root.enter_context(self.nc.semaphore("d_" + key))
            self.dma_cnt[key] = 0
        return self.dma_sems[key]

    def _need(self, eng, tag, waits):
        if tag is None:
            return
        key, val = tag
        if key == eng and eng not in self.same:
            return
        if self.known[eng].get(key, 0) >= val:
            return
        self.known[eng][key] = val
        waits[key] = max(waits.get(key, 0), val)

    def _deps(self, eng, reads, writes):
        waits = {}
        excl = [b for b in reads if b.excl]
        for b in reads:
            self._need(eng, b.w, waits)
        for b in list(writes) + excl:
            self._need(eng, b.w, waits)
            for t in b.r:
                self._need(eng, t, waits)
        return waits

    def _semobj(self, key):
        return self.sems[key] if key in self.sems else self.dma_sems[key]

    def _mark(self, tag, reads, writes):
        for b in reads:
            if b.excl:
                b.w = tag
                b.r = []
            else:
                b.r.append(tag)
        for b in writes:
            b.w = tag
            b.r = []

    def op(self, eng, fn, reads=(), writes=()):
        waits = self._deps(eng, reads, writes)
        self.cnt[eng] += 1
        tag = (eng, self.cnt[eng])
        fn = _freeze(fn)
        self.prog[eng].append((fn, [(self._semobj(k), v) for k, v in waits.items()], (self.sems[eng], 1)))
        self._mark(tag, reads, writes)
        for b in reads:
            if not b.excl and len(b.r) > 8:
                last = {}
                for k, v in b.r:
                    last[k] = max(last.get(k, 0), v)
                b.r = list(last.items())
        return tag

    def dma(self, q, out, in_, semkey, reads=(), writes=()):
        waits = self._deps(q, reads, writes)
        sem = self.dma_sem(semkey)
        if self.dma_cnt[semkey] > 0:
            self._need(q, (semkey, self.dma_cnt[semkey]), waits)
        self.dma_cnt[semkey] += 16
        tag = (semkey, self.dma_cnt[semkey])

        def fn(e, out=out, in_=in_):
            return e.dma_start(out=out, in_=in_)
        self.prog[q].append((fn, [(self._semobj(k), v) for k, v in waits.items()], (sem, 16)))
        self._mark(tag, reads, writes)
        return tag

    def barrier(self):
        for e in self.ENG:
            waits = {}
            for o in self.ENG:
                if o != e and self.cnt[o] > 0:
                    self._need(e, (o, self.cnt[o]), waits)
            for k, v in self.dma_cnt.items():
                if v > 0:
                    self._need(e, (k, v), waits)
            if waits:
                self.prog[e].append((None, [(self._semobj(k), v) for k, v in waits.items()], None))

    def emit(self):
        nc = self.nc
        with nc.Block() as block:
            def run(ename):
                def body(e):
                    for fn, waits, inc in self.prog[ename]:
                        for s, v in waits:
                            e.wait_ge(s, v)
                        if fn is None:
                            continue
                        ins = fn(e)
                        if inc is not None:
                            ins.then_inc(inc[0], inc[1])
                return body
            block.tensor(run("pe"))
            block.scalar(run("act"))
            block.vector(run("dve"))
            block.gpsimd(run("pool"))
            block.sync(run("sp"))


def build_program():
    nc = bass.Bass("TRN2", target_bir_lowering=False)

    def din(name, shape):
        return nc.dram_tensor(name, list(shape), F32, kind="ExternalInput").ap()

    def dout(name, shape):
        return nc.dram_tensor(name, list(shape), F32, kind="ExternalOutput").ap()

    xT = din("xT", [D, NT])
    cT = din("cT", [D, NG])
    conv_st = din("conv_st", [L, 512, NB, 3])
    h_st = din("h_st", [L, 512, NB])
    shift_st = din("shift_st", [L, 1792, NB])
    wkv_st = din("wkv_st", [L, NB, 8, 64, 64])
    w_ada = din("w_ada", [L, D, 9 * D])
    b_adaT = din("b_adaT", [L, 128, 72])
    normsT = din("normsT", [L, 3, 128, 8])
    fnormT = din("fnormT", [128, 8])
    wg_d = [din("ffn1_w_gate", [L, D, DFF]), din("ffn2_w_gate", [L, D, DFF])]
    wu_d = [din("ffn1_w_up", [L, D, DFF]), din("ffn2_w_up", [L, D, DFF])]
    wd_d = [din("ffn1_w_down", [L, DFF, D]), din("ffn2_w_down", [L, DFF, D])]
    w_in = din("w_in", [L, D, DFF])
    w_out = din("w_out", [L, D, D])
    bd_gate = din("bd_gate", [L, 2, 4, 128, 128])
    lru_vec = din("lru_vec", [L, 128, 8, 4])
    rw_vec = din("rw_vec", [L, 128, 7, 4])
    mu_T = din("mu_T", [L, 128, 14])
    w2a2 = din("w2a2", [L, 128, 512])
    g2_d = din("g2", [L, 128, 512])
    consts = din("consts", [128, 9, 128])
    rmask_d = din("rmask", [128, 320])

    yT = dout("yT", [D, NT])
    conv_out = dout("conv_out", [L, 512, NG, 3])
    h_out = dout("h_out", [L, 512, NG])
    shift_out = dout("shift_out", [L, 1792, NG])
    wkv_out = dout("wkv_out", [L, NG, 8, 64, 64])

    xscr = nc.dram_tensor("xscr", [D, NT], F32).ap()

    with ExitStack() as st:
        S = Sched(nc, st, same_engine_sync=(("pool",) if os.environ.get("MK_NOSAME") else ("pool", "act", "dve")))
        op = S.op
        R = {}

        Bx = [S.buf(f"x{i}") for i in range(8)]
        hT = S.sb([128, 8, NT], BF16); Bh = S.buf("h")
        big = S.sb([128, 8 * NT], BF16); Bbig = S.buf("big")
        cst = S.sb([128, 9, 128]); Bc = S.buf("c")
        rmask = S.sb([128, 320]); Brm = S.buf("rm")
        ones_bf = S.sb([128, 128], BF16)
        eps_t = S.sb([128, 4])
        mod = [S.sb([128, 72, NG]) for _ in range(L)]; Bmod = [[S.buf() for _ in range(3)] for _ in range(L)]
        b_ada_t = S.sb([128, L, 72])
        norms_t = S.sb([128, L, 3, 8])
        fnorm_t = S.sb([128, 8])
        fsh_t = S.sb([128, 8, NG])
        fA_t = S.sb([128, 8, NG])
        Bsm = S.buf("smallconst")

        ident = cst[:, 0, :]
        blk1 = cst[:, 1, :]
        m4 = cst[:, 2:6, :]
        mlow = cst[:, 6, :]

        banks = [S.ps([128, 512]) for _ in range(8)]
        Bbank = [S.buf(f"bank{i}", excl=True) for i in range(8)]
        ring = {"i": 0}

        def pbank():
            i = ring["i"]
            ring["i"] = (i + 1) % 8
            return banks[i], Bbank[i]

        NSLOT = 4
        SLOT = 8 * 128
        stg = [S.sb([128, SLOT]) for _ in range(NSLOT)]; Bstg = [S.buf() for _ in range(NSLOT)]
        wbf = [S.sb([128, SLOT], BF16) for _ in range(NSLOT)]; Bwbf = [S.buf() for _ in range(NSLOT)]
        wr = {"i": 0}

        def load_w(dram_ap, nk, ncol=128):
            i = wr["i"]
            wr["i"] = (i + 1) % NSLOT
            n = nk * ncol
            assert n <= SLOT
            sv = stg[i][:, 0:n].rearrange("p (k c) -> p k c", c=ncol)
            S.dma("sp", sv, dram_ap, f"stg{i}", writes=[Bstg[i]])
            op("pool", lambda e: e.tensor_copy(out=wbf[i][:, 0:n], in_=stg[i][:, 0:n]), reads=[Bstg[i]], writes=[Bwbf[i]])
            return wbf[i][:, 0:n].rearrange("p (k c) -> p k c", c=ncol), Bwbf[i]

        def wslice(w2d, col0, ncol=128):
            return w2d.rearrange("(kc p) n -> p kc n", p=128)[:, :, col0:col0 + ncol]

        class Scope:
            def __enter__(self):
                self.es = ExitStack()
                self.prev = S.stack
                S.stack = self.es
                return self

            def __exit__(self, *a):
                S.barrier()
                S.stack = self.prev
                self.es.close()
                return False

        S.dma("sp", cst[:], consts, "c", writes=[Bc])
        S.dma("sp", rmask[:], rmask_d, "rm", writes=[Brm])
        S.dma("sp", b_ada_t[:], b_adaT.rearrange("l p j -> p l j"), "sm", writes=[Bsm])
        S.dma("sp", norms_t[:], normsT.rearrange("l s p k -> p l s k"), "sm", writes=[Bsm])
        S.dma("sp", fnorm_t[:], fnormT, "sm", writes=[Bsm])
        op("pool", lambda e: e.memset(ones_bf[:], 1.0), writes=[Bsm])
        op("pool", lambda e: e.memset(eps_t[:, 0:1], NORM_EPS), writes=[Bsm])
        op("pool", lambda e: e.memset(eps_t[:, 1:2], 1.0), writes=[Bsm])
        op("pool", lambda e: e.memset(eps_t[:, 2:3], GN_EPS), writes=[Bsm])
        op("pool", lambda e: e.memset(eps_t[:, 3:4], 0.0), writes=[Bsm])
        op("pool", lambda e: e.memset(fsh_t[:], 0.0), writes=[Bsm])
        op("dve", lambda e: e.tensor_copy(out=fA_t[:], in_=fnorm_t[:, :].unsqueeze(2).broadcast_to([128, 8, NG])), reads=[Bsm], writes=[Bsm])

        cs32 = S.sb([128, 8, NG]); csb = S.sb([128, 8, NG], BF16); Bcs = S.buf()
        S.dma("sp", cs32[:], cT.rearrange("(k p) g -> p k g", p=128), "sm", writes=[Bcs])
        op("act", lambda e: e.activation(out=csb[:], in_=cs32[:], func=AF.Silu), reads=[Bcs], writes=[Bcs])

        def ada_chunk(l, j, wt, Bw):
            pb, Bp = pbank()
            for kc in range(8):
                op("pe", lambda e: e.matmul(pb[:, 0:NG], lhsT=wt[:, kc, :], rhs=csb[:, kc, :], start=(kc == 0), stop=(kc == 7)), reads=[Bw, Bcs], writes=[Bp])
            op("act", lambda e: e.activation(out=mod[l][:, j, :], in_=pb[:, 0:NG], func=AF.Identity, bias=b_ada_t[:, l, j:j + 1]),
               reads=[Bp, Bsm], writes=[Bmod[l][j // 24]])

        def ada_post(l, s_):
            Bm = Bmod[l][s_]
            scv = mod[l][:, (3 * s_ + 1) * 8:(3 * s_ + 2) * 8, :]
            op("dve", lambda e: e.tensor_scalar(out=scv, in0=scv, scalar1=1.0, scalar2=None, op0=ALU.add), reads=[Bm], writes=[Bm])
            op("dve", lambda e: e.tensor_tensor(out=scv, in0=scv, in1=norms_t[:, l, s_, :].unsqueeze(2).broadcast_to([128, 8, NG]), op=ALU.mult),
               reads=[Bm, Bsm], writes=[Bm])
            if s_ != 1:
                gv = mod[l][:, (3 * s_ + 2) * 8:(3 * s_ + 3) * 8, :]
                op("dve", lambda e: e.tensor_scalar(out=gv, in0=gv, scalar1=0.5, scalar2=None, op0=ALU.mult), reads=[Bm], writes=[Bm])

        pend = [load_w(wslice(w_ada[0], j * 128), 8) for j in range(2)]
        for j in range(24):
            wt, Bw = pend.pop(0)
            if j + 2 < 24:
                pend.append(load_w(wslice(w_ada[0], (j + 2) * 128), 8))
            ada_chunk(0, j, wt, Bw)
        ada_post(0, 0)

        def ada_stream(astg, Bas, awb, Baw, seq, posts):

            def ld(k):
                l, j = seq[k]
                i = k % 2
                S.dma("sp", astg[i][:, :].rearrange("p (k c) -> p k c", c=128), wslice(w_ada[l], j * 128), f"astg{i}", writes=[Bas[i]])
                op("pool", lambda e: e.tensor_copy(out=awb[i][:], in_=astg[i][:]), reads=[Bas[i]], writes=[Baw[i]])
            ld(0)
            for k, (l, j) in enumerate(seq):
                if k + 1 < len(seq):
                    ld(k + 1)
                ada_chunk(l, j, awb[k % 2][:, :].rearrange("p (k c) -> p k c", c=128), Baw[k % 2])
                yield
            for (l, s_) in posts:
                ada_post(l, s_)

        def ffn_with_stream(l, which, G, first, seq, posts):
            with Scope():
                astg = [S.sb([128, SLOT]) for _ in range(2)]; Bas = [S.buf() for _ in range(2)]
                awb = [S.sb([128, SLOT], BF16) for _ in range(2)]; Baw = [S.buf() for _ in range(2)]
                ag = ada_stream(astg, Bas, awb, Baw, seq, posts)
                R["ffn_fill"] = lambda: next(ag, None)
                ffn(l, which, G, first=first)
                R["ffn_fill"] = lambda: None
                for _ in ag:
                    pass

        def bc4(ap3):
            return ap3.unsqueeze(3).broadcast_to([128, ap3.shape[1], NB, 4])

        NT256 = [(i * 256, 256) for i in range(8)] + [(2048, 64)]

        def norm_mod(A, SH, Bpar, out_fn, after=None, nbuf=2):
            x = R["x"]
            with Scope():
                t8s = [(S.sb([128, 8, 256]), S.buf()) for _ in range(nbuf)]
                sqbs = [(S.sb([128, 8, 256], BF16), S.buf()) for _ in range(nbuf)]
                rstds = [(S.sb([128, 256]), S.buf()) for _ in range(nbuf)]
                pbs = {}

                def stA(ti):
                    c0, n = NT256[ti]
                    sqb, Bsq = sqbs[ti % nbuf]
                    op("dve", lambda e: e.tensor_tensor(out=sqb[:, :, 0:n], in0=x[:, :, c0:c0 + n], in1=x[:, :, c0:c0 + n], op=ALU.mult), reads=Bx, writes=[Bsq])
                    pb, Bp = pbank()
                    pbs[ti] = (pb, Bp)
                    for kc in range(8):
                        op("pe", lambda e: e.matmul(pb[:, 0:n], lhsT=ones_bf[:], rhs=sqb[:, kc, 0:n], start=(kc == 0), stop=(kc == 7)), reads=[Bsq, Bsm], writes=[Bp])

                def stB(ti):
                    c0, n = NT256[ti]
                    t8, BtmpA = t8s[ti % nbuf]; rstd, Brs = rstds[ti % nbuf]
                    pb, Bp = pbs.pop(ti)
                    op("act", lambda e: e.activation(out=rstd[:, 0:n], in_=pb[:, 0:n], func=AF.Ln, bias=eps_t[:, 0:1], scale=1.0 / D), reads=[Bp, Bsm], writes=[Brs])
                    op("act", lambda e: e.activation(out=rstd[:, 0:n], in_=rstd[:, 0:n], func=AF.Exp, scale=-0.5), reads=[Brs], writes=[Brs])
                    op("dve", lambda e: e.tensor_tensor(out=t8[:, :, 0:n], in0=x[:, :, c0:c0 + n], in1=rstd[:, 0:n].unsqueeze(1).broadcast_to([128, 8, n]), op=ALU.mult),
                       reads=Bx + [Brs], writes=[BtmpA])

                def stC(ti):
                    c0, n = NT256[ti]
                    t8, BtmpA = t8s[ti % nbuf]
                    o, Bo = out_fn(ti, c0, n)
                    if c0 < NP_:
                        for kc in range(8):
                            op("act", lambda e: e.activation(out=o[:, kc, :], in_=t8[:, kc, 0:n], func=AF.Identity, scale=A[:, kc, 0:1], bias=SH[:, kc, 0:1]),
                               reads=[BtmpA, Bpar], writes=[Bo])
                    else:
                        t4 = t8[:, :, 0:n].rearrange("p k (b t) -> p k b t", t=4)
                        op("dve", lambda e: e.tensor_tensor(out=t4, in0=t4, in1=bc4(A[:, :, 1:NG]), op=ALU.mult), reads=[BtmpA, Bpar], writes=[BtmpA])
                        op("dve", lambda e: e.tensor_tensor(out=o.rearrange("p k (b t) -> p k b t", t=4), in0=t4, in1=bc4(SH[:, :, 1:NG]), op=ALU.add),
                           reads=[BtmpA, Bpar], writes=[Bo])
                    if after is not None:
                        after(ti, c0, n)

                nt = len(NT256)
                if nbuf < 2:
                    for ti in range(nt):
                        stA(ti); stB(ti); stC(ti)
                else:
                    stA(0)
                    for ti in range(nt):
                        if ti + 1 < nt:
                            stA(ti + 1)
                        stB(ti)
                        stC(ti)

        def h_out_fn(ti, c0, n):
            return hT[:, :, c0:c0 + n], Bh

        def resid_add(pb, Bp, i, c0, n, G, Bpar):
            x = R["x"]; ptmp = R["ptmp"]; Bpt = R["Bpt"]
            if c0 < NP_:
                op("dve", lambda e: e.scalar_tensor_tensor(out=x[:, i, c0:c0 + n], in0=pb[:, 0:n], scalar=G[:, i, 0:1], in1=x[:, i, c0:c0 + n], op0=ALU.mult, op1=ALU.add),
                   reads=[Bp, Bpar, Bx[i]], writes=[Bx[i]])
            else:
                op("dve", lambda e: e.tensor_tensor(out=ptmp[:, 0:n].rearrange("p (b t) -> p b t", t=4), in0=pb[:, 0:n].rearrange("p (b t) -> p b t", t=4),
                                                    in1=G[:, i, 1:NG].unsqueeze(2).broadcast_to([128, NB, 4]), op=ALU.mult), reads=[Bp, Bpar], writes=[Bpt])
                op("dve", lambda e: e.tensor_tensor(out=x[:, i, c0:c0 + n], in0=x[:, i, c0:c0 + n], in1=ptmp[:, 0:n], op=ALU.add), reads=[Bpt, Bx[i]], writes=[Bx[i]])

        FPARTS = [(0, 8), (8, 7), (15, 7)]

        def ffn_prefetch(l, which):
            return [(load_w(wslice(wg_d[which][l], 0), 8), load_w(wslice(wu_d[which][l], 0), 8))]

        def ffn(l, which, G, first=None):
            act3 = big[:, :].rearrange("p (j t) -> p j t", t=NT)
            with Scope():
                sgt = [S.sb([128, 512]) for _ in range(2)]; Bsg = [S.buf() for _ in range(2)]
                for (j0, nj) in FPARTS:
                    jlist = list(range(j0, j0 + nj))
                    if j0 == 0 and first is not None:
                        pend = first
                    else:
                        pend = [(load_w(wslice(wg_d[which][l], jlist[0] * 128), 8), load_w(wslice(wu_d[which][l], jlist[0] * 128), 8))]
                    for jj, j in enumerate(jlist):
                        (wg, Bwg), (wu, Bwu) = pend.pop(0)
                        if jj + 1 < nj:
                            j2 = jlist[jj + 1]
                            pend.append((load_w(wslice(wg_d[which][l], j2 * 128), 8), load_w(wslice(wu_d[which][l], j2 * 128), 8)))
                        for ti, (c0, n) in enumerate(TT):
                            pg, Bpg = pbank()
                            pu, Bpu = pbank()
                            for kc in range(8):
                                op("pe", lambda e, kc=kc, pg=pg, wg=wg, c0=c0, n=n: e.matmul(pg[:, 0:n], lhsT=wg[:, kc, :], rhs=hT[:, kc, c0:c0 + n], start=(kc == 0), stop=(kc == 7)),
                                   reads=[Bwg, Bh], writes=[Bpg])
                            for kc in range(8):
                                op("pe", lambda e, kc=kc, pu=pu, wu=wu, c0=c0, n=n: e.matmul(pu[:, 0:n], lhsT=wu[:, kc, :], rhs=hT[:, kc, c0:c0 + n], start=(kc == 0), stop=(kc == 7)),
                                   reads=[Bwu, Bh], writes=[Bpu])
                            sg, Bs_ = sgt[ti % 2], Bsg[ti % 2]
                            op("act", lambda e, pg=pg, sg=sg, n=n: e.activation(out=sg[:, 0:n], in_=pg[:, 0:n], func=AF.Silu), reads=[Bpg], writes=[Bs_])
                            op("dve", lambda e, pu=pu, sg=sg, jj=jj, c0=c0, n=n: e.tensor_tensor(out=act3[:, jj, c0:c0 + n], in0=pu[:, 0:n], in1=sg[:, 0:n], op=ALU.mult),
                               reads=[Bpu, Bs_], writes=[Bbig])
                            R.get("ffn_fill", lambda: None)()
                    wdv = wd_d[which][l].rearrange("(fc p) n -> p fc n", p=128)
                    pend = [load_w(wdv[:, j0:j0 + nj, 0:128], nj)]
                    for i in range(8):
                        wd, Bwd = pend.pop(0)
                        if i + 1 < 8:
                            pend.append(load_w(wdv[:, j0:j0 + nj, (i + 1) * 128:(i + 2) * 128], nj))
                        for ti, (c0, n) in enumerate(TT):
                            pb, Bp = pbank()
                            for jj in range(nj):
                                op("pe", lambda e, jj=jj, pb=pb, wd=wd, c0=c0, n=n: e.matmul(pb[:, 0:n], lhsT=wd[:, jj, :], rhs=act3[:, jj, c0:c0 + n], start=(jj == 0), stop=(jj == nj - 1)),
                                   reads=[Bwd, Bbig], writes=[Bp])
                            resid_add(pb, Bp, i, c0, n, G, Bmod[l][2 * which])
                            R.get("ffn_fill", lambda: None)()

        ymix = big[:, 0:8 * NT].rearrange("p (j t) -> p j t", t=NT)
        MT = [(i * 256, 256) for i in range(8)] + [(2048, 64)]
        LASTP = 7
        SMP = 8
        SW = 264

        def mixer(l):
            lv = S.sb([128, 8, 4]); rv = S.sb([128, 7, 4]); muv = S.sb([128, 14]); Bpv = S.buf()
            S.dma("sp", lv[:], lru_vec[l], "sm", writes=[Bpv])
            S.dma("sp", rv[:], rw_vec[l], "sm", writes=[Bpv])
            S.dma("sp", muv[:], mu_T[l], "sm", writes=[Bpv])
            lsc = S.sb([128, 3, 4])
            op("act", lambda e: e.activation(out=lsc[:, 0, :], in_=lv[:, 7, :], func=AF.Exp, scale=-1.0), reads=[Bpv], writes=[Bpv])
            op("act", lambda e: e.activation(out=lsc[:, 0, :], in_=lsc[:, 0, :], func=AF.Ln, bias=eps_t[:, 1:2], scale=1.0), reads=[Bpv, Bsm], writes=[Bpv])
            op("dve", lambda e: e.tensor_scalar(out=lsc[:, 1, :], in0=lsc[:, 0, :], scalar1=-8.0, scalar2=None, op0=ALU.mult), reads=[Bpv], writes=[Bpv])
            op("dve", lambda e: e.tensor_scalar(out=lsc[:, 2, :], in0=lsc[:, 0, :], scalar1=-16.0, scalar2=None, op0=ALU.mult), reads=[Bpv], writes=[Bpv])
            bdw = S.sb([128, 8 * 128], BF16); Bbd = S.buf()
            for q in range(2):
                wt, Bw = load_w(bd_gate[l, q].rearrange("c p n -> p c n"), 4)
                op("pool", lambda e, q=q, wt=wt: e.tensor_copy(out=bdw[:, q * 512:(q + 1) * 512].rearrange("p (c n) -> p c n", n=128), in_=wt), reads=[Bw], writes=[Bbd])
            bd4 = bdw[:, :].rearrange("p (q c n) -> p q c n", q=2, c=4)
            w2a2b = S.sb([128, 512], BF16); g2b = S.sb([128, 512], BF16); Blr = S.buf()
            wt, Bw = load_w(w2a2[l].rearrange("p (k n) -> p k n", k=1), 1, 512)
            op("pool", lambda e, wt=wt: e.tensor_copy(out=w2a2b[:], in_=wt[:, 0, :]), reads=[Bw], writes=[Blr])
            wt, Bw = load_w(g2_d[l].rearrange("p (k n) -> p k n", k=1), 1, 512)
            op("pool", lambda e, wt=wt: e.tensor_copy(out=g2b[:], in_=wt[:, 0, :]), reads=[Bw], writes=[Blr])

            conv_o = S.sb([128, 4, NG, 3]); hO = S.sb([128, 4, NG]); shO = S.sb([128, 14, NG]); Bout = S.buf()
            conv_i = S.sb([128, 4, NB, 3]); hI = S.sb([128, 4, NB]); shI = S.sb([128, 14, NB]); Bin = S.buf()
            S.dma("sp", conv_i[:], conv_st[l].rearrange("(c p) b j -> p c b j", p=128), "sm", writes=[Bin])
            S.dma("sp", hI[:], h_st[l].rearrange("(c p) b -> p c b", p=128), "sm", writes=[Bin])
            S.dma("sp", shI[:], shift_st[l].rearrange("(c p) b -> p c b", p=128), "sm", writes=[Bin])

            def proj(ws, Bws, c0, n):
                pb, Bp = pbank()
                for kc in range(8):
                    op("pe", lambda e, kc=kc, pb=pb: e.matmul(pb[:, 0:n], lhsT=ws[:, kc, :], rhs=hT[:, kc, c0:c0 + n], start=(kc == 0), stop=(kc == 7)),
                       reads=[Bws, Bh], writes=[Bp])
                return pb, Bp

            LR1 = S.sb([128, NT], BF16); LR2 = S.sb([128, NT], BF16); BLRs = [S.buf(), S.buf()]
            BLR = BLRs[0]
            with Scope():
                Ts = [{k: (S.sb([128, SW]), S.buf()) for k in ["xpad", "xc", "gbr", "gx", "ga", "a", "bb", "hs", "u"]} for _ in range(4)]
                xcbs = [(S.sb([128, 256], BF16), S.buf()) for _ in range(4)]
                cars = [(S.sb([128, 3]), S.sb([128, 1]), S.buf()) for _ in range(4)]
                wxs = []; wgs = []
                for c in range(4):
                    for (lst, col0) in ((wxs, c * 128), (wgs, 512 + c * 128)):
                        wt, Bw = load_w(wslice(w_in[l], col0), 8)
                        dt_ = S.sb([128, 8, 128], BF16); Bd_ = S.buf()
                        op("pool", lambda e: e.tensor_copy(out=dt_[:], in_=wt), reads=[Bw], writes=[Bd_])
                        lst.append((dt_, Bd_))
                def m1_stream(c):
                    T = Ts[c]; xcb, Bxcb = xcbs[c]; carry3, hprev, Bcar = cars[c]
                    wx, Bwx = wxs[c]; wgt, Bwg = wgs[c]
                    op("pool", lambda e: e.memset(carry3[:], 0.0), writes=[Bcar])
                    op("pool", lambda e: e.memset(hprev[:], 0.0), writes=[Bcar])
                    cw = [lv[:, j, c:c + 1] for j in range(4)]
                    for ti, (c0, n) in enumerate(MT):
                        smp = ti == SMP
                        xp, Bxp = T["xpad"]; xc, Bxc = T["xc"]; gbr, Bgb = T["gbr"]; gx, Bgx = T["gx"]; ga, Bga = T["ga"]
                        a_, Ba = T["a"]; bb, Bbb = T["bb"]; hs, Bhs = T["hs"]; u_, Bu = T["u"]
                        pbx, Bpx = proj(wx, Bwx, c0, n)
                        if not smp:
                            op("act", lambda e: e.activation(out=xp[:, 3:3 + n], in_=pbx[:, 0:n], func=AF.Copy), reads=[Bpx], writes=[Bxp])
                            yield
                            op("dve", lambda e: e.tensor_copy(out=xp[:, 0:3], in_=carry3[:]), reads=[Bcar], writes=[Bxp])
                            yield
                            op("dve", lambda e: e.tensor_copy(out=carry3[:], in_=xp[:, n:n + 3]), reads=[Bxp], writes=[Bcar])
                            yield
                            xv = [xp[:, j:j + n] for j in range(4)]
                            xcv = xc[:, 0:n]
                        else:
                            xp3 = xp[:, 0:NB * 7].rearrange("p (b j) -> p b j", j=7)
                            op("act", lambda e: e.activation(out=xp3[:, :, 3:7], in_=pbx[:, 0:n].rearrange("p (b t) -> p b t", t=4), func=AF.Copy), reads=[Bpx], writes=[Bxp])
                            yield
                            op("dve", lambda e: e.tensor_copy(out=xp3[:, :, 0:3], in_=conv_i[:, c, :, :]), reads=[Bin], writes=[Bxp])
                            yield
                            xv = [xp3[:, :, j:j + 4] for j in range(4)]
                            xcv = xc[:, 0:n].rearrange("p (b t) -> p b t", t=4)
                        pbg_, Bpg_ = proj(wgt, Bwg, c0, n)
                        op("act", lambda e: e.activation(out=gbr[:, 0:n], in_=pbg_[:, 0:n], func=AF.Copy), reads=[Bpg_], writes=[Bgb])
                        yield
                        op("dve", lambda e: e.tensor_scalar(out=xcv, in0=xv[3], scalar1=cw[3], scalar2=lv[:, 4, c:c + 1], op0=ALU.mult, op1=ALU.add), reads=[Bxp, Bpv], writes=[Bxc])
                        yield
                        for j in range(3):
                            op("dve", lambda e, j=j: e.scalar_tensor_tensor(out=xcv, in0=xv[j], scalar=cw[j], in1=xcv, op0=ALU.mult, op1=ALU.add), reads=[Bxp, Bpv, Bxc], writes=[Bxc])
                            yield
                        op("act", lambda e: e.activation(out=xcb[:, 0:n], in_=xc[:, 0:n], func=AF.Copy), reads=[Bxc], writes=[Bxcb])
                        yield
                        if ti == LASTP:
                            op("pool", lambda e: e.tensor_copy(out=conv_o[:, c, 0, :], in_=xp[:, n:n + 3]), reads=[Bxp], writes=[Bout])
                            yield
                        if smp:
                            op("pool", lambda e: e.tensor_copy(out=conv_o[:, c, 1:NG, :], in_=xp3[:, :, 4:7]), reads=[Bxp], writes=[Bout])
                            yield
                        p1, Bp1 = pbank()
                        op("pe", lambda e: e.matmul(p1[:, 0:n], lhsT=bd4[:, 0, c, :], rhs=xcb[:, 0:n], start=True, stop=True), reads=[Bbd, Bxcb], writes=[Bp1])
                        p2, Bp2 = pbank()
                        op("pe", lambda e: e.matmul(p2[:, 0:n], lhsT=bd4[:, 1, c, :], rhs=xcb[:, 0:n], start=True, stop=True), reads=[Bbd, Bxcb], writes=[Bp2])
                        op("act", lambda e: e.activation(out=gx[:, 0:n], in_=p1[:, 0:n], func=AF.Sigmoid, bias=lv[:, 5, c:c + 1]), reads=[Bp1, Bpv], writes=[Bgx])
                        op("act", lambda e: e.activation(out=ga[:, 0:n], in_=p2[:, 0:n], func=AF.Sigmoid, bias=lv[:, 6, c:c + 1]), reads=[Bp2, Bpv], writes=[Bga])
                        yield
                        op("act", lambda e: e.activation(out=a_[:, 0:n], in_=ga[:, 0:n], func=AF.Exp, scale=lsc[:, 1, c:c + 1]), reads=[Bga, Bpv], writes=[Ba])
                        yield
                        op("act", lambda e: e.activation(out=bb[:, 0:n], in_=ga[:, 0:n], func=AF.Exp, scale=lsc[:, 2, c:c + 1]), reads=[Bga, Bpv], writes=[Bbb])
                        yield
                        op("act", lambda e: e.activation(out=bb[:, 0:n], in_=bb[:, 0:n], func=AF.Sqrt, bias=eps_t[:, 1:2], scale=-1.0), reads=[Bbb, Bsm], writes=[Bbb])
                        yield
                        op("dve", lambda e: e.tensor_tensor(out=bb[:, 0:n], in0=bb[:, 0:n], in1=gx[:, 0:n], op=ALU.mult), reads=[Bbb, Bgx], writes=[Bbb])
                        yield
                        op("dve", lambda e: e.tensor_tensor(out=bb[:, 0:n], in0=bb[:, 0:n], in1=xc[:, 0:n], op=ALU.mult), reads=[Bbb, Bxc], writes=[Bbb])
                        yield
                        if not smp:
                            op("dve", lambda e: e.tensor_tensor_scan(out=hs[:, 0:n], data0=a_[:, 0:n], data1=bb[:, 0:n], initial=hprev[:, 0:1], op0=ALU.mult, op1=ALU.add),
                               reads=[Ba, Bbb, Bcar], writes=[Bhs])
                            yield
                            op("dve", lambda e: e.tensor_copy(out=hprev[:], in_=hs[:, n - 1:n]), reads=[Bhs], writes=[Bcar])
                            yield
                            if ti == LASTP:
                                op("pool", lambda e: e.tensor_copy(out=hO[:, c, 0:1], in_=hs[:, n - 1:n]), reads=[Bhs], writes=[Bout])
                                yield
                        else:
                            a3 = a_[:, 0:n].rearrange("p (b t) -> p b t", t=4)
                            b3 = bb[:, 0:n].rearrange("p (b t) -> p b t", t=4)
                            op("dve", lambda e: e.tensor_tensor(out=u_[:, 0:NB], in0=a3[:, :, 0], in1=hI[:, c, :], op=ALU.mult), reads=[Ba, Bin], writes=[Bu])
                            yield
                            op("dve", lambda e: e.tensor_tensor(out=b3[:, :, 0], in0=b3[:, :, 0], in1=u_[:, 0:NB], op=ALU.add), reads=[Bu, Bbb], writes=[Bbb])
                            yield
                            op("dve", lambda e: e.memset(a3[:, :, 0], 0.0), reads=[Bu], writes=[Ba])
                            yield
                            op("dve", lambda e: e.tensor_tensor_scan(out=hs[:, 0:n], data0=a_[:, 0:n], data1=bb[:, 0:n], initial=0.0, op0=ALU.mult, op1=ALU.add),
                               reads=[Ba, Bbb], writes=[Bhs])
                            yield
                            op("pool", lambda e: e.tensor_copy(out=hO[:, c, 1:NG], in_=hs[:, 0:n].rearrange("p (b t) -> p b t", t=4)[:, :, 3]), reads=[Bhs], writes=[Bout])
                            yield
                        op("pool", lambda e: e.tensor_tensor(out=u_[:, 0:n], in0=gbr[:, 0:n], in1=gbr[:, 0:n], op=ALU.mult), reads=[Bgb], writes=[Bu])
                        yield
                        op("dve", lambda e: e.tensor_scalar(out=u_[:, 0:n], in0=u_[:, 0:n], scalar1=0.044715, scalar2=1.0, op0=ALU.mult, op1=ALU.add), reads=[Bu], writes=[Bu])
                        yield
                        op("dve", lambda e: e.tensor_tensor(out=u_[:, 0:n], in0=u_[:, 0:n], in1=gbr[:, 0:n], op=ALU.mult), reads=[Bu, Bgb], writes=[Bu])
                        yield
                        op("act", lambda e: e.activation(out=u_[:, 0:n], in_=u_[:, 0:n], func=AF.Sigmoid, scale=1.5957691216057308), reads=[Bu], writes=[Bu])
                        yield
                        op("dve", lambda e: e.tensor_tensor(out=u_[:, 0:n], in0=u_[:, 0:n], in1=gbr[:, 0:n], op=ALU.mult), reads=[Bu, Bgb], writes=[Bu])
                        yield
                        op("dve", lambda e: e.tensor_tensor(out=ymix[:, c, c0:c0 + n], in0=u_[:, 0:n], in1=hs[:, 0:n], op=ALU.mult), reads=[Bu, Bhs], writes=[Bbig])
                        yield

                def m2_stream(jch):
                    wt, Bw = load_w(wslice(w_in[l], 1024 + jch * 128), 8)
                    ws = S.sb([128, 8, 128], BF16); Bws = S.buf()
                    op("pool", lambda e: e.tensor_copy(out=ws[:], in_=wt), reads=[Bw], writes=[Bws])
                    prp = S.sb([128, SW]); Bprp = S.buf()
                    dtm = S.sb([128, 256]); Bdt = S.buf()
                    xs0 = S.sb([128, 256]); Bxs0 = S.buf()
                    car = S.sb([128, 1]); Bca = S.buf()
                    Bl = BLRs[jch - 12]
                    op("pool", lambda e: e.memset(car[:], 0.0), writes=[Bca])
                    yield
                    for ti, (c0, n) in enumerate(MT):
                        pb, Bp = proj(ws, Bws, c0, n)
                        if ti != SMP:
                            op("act", lambda e: e.activation(out=prp[:, 1:1 + n], in_=pb[:, 0:n], func=AF.Copy), reads=[Bp], writes=[Bprp])
                            yield
                            op("dve", lambda e: e.tensor_copy(out=prp[:, 0:1], in_=car[:]), reads=[Bca], writes=[Bprp])
                            op("dve", lambda e: e.tensor_copy(out=car[:], in_=prp[:, n:n + 1]), reads=[Bprp], writes=[Bca])
                            if ti == LASTP:
                                op("pool", lambda e: e.tensor_copy(out=shO[:, jch, 0:1], in_=prp[:, n:n + 1]), reads=[Bprp], writes=[Bout])
                            yield
                            cur, prev, dv, ov = prp[:, 1:1 + n], prp[:, 0:n], dtm[:, 0:n], xs0[:, 0:n]
                        else:
                            p3 = prp[:, 0:NB * 5].rearrange("p (b j) -> p b j", j=5)
                            op("act", lambda e: e.activation(out=p3[:, :, 1:5], in_=pb[:, 0:n].rearrange("p (b t) -> p b t", t=4), func=AF.Copy), reads=[Bp], writes=[Bprp])
                            yield
                            op("dve", lambda e: e.tensor_copy(out=p3[:, :, 0], in_=shI[:, jch, :]), reads=[Bin], writes=[Bprp])
                            op("pool", lambda e: e.tensor_copy(out=shO[:, jch, 1:NG], in_=p3[:, :, 4]), reads=[Bprp], writes=[Bout])
                            yield
                            cur, prev = p3[:, :, 1:5], p3[:, :, 0:4]
                            dv = dtm[:, 0:n].rearrange("p (b t) -> p b t", t=4)
                            ov = xs0[:, 0:n].rearrange("p (b t) -> p b t", t=4)
                        op("dve", lambda e: e.tensor_tensor(out=dv, in0=prev, in1=cur, op=ALU.subtract), reads=[Bprp], writes=[Bdt])
                        yield
                        op("dve", lambda e: e.scalar_tensor_tensor(out=ov, in0=dv, scalar=muv[:, jch:jch + 1], in1=cur, op0=ALU.mult, op1=ALU.add), reads=[Bdt, Bprp, Bpv], writes=[Bxs0])
                        yield
                        if jch == 12:
                            op("act", lambda e: e.activation(out=LR1[0:64, c0:c0 + n], in_=xs0[0:64, 0:n], func=AF.Tanh), reads=[Bxs0], writes=[Bl])
                            op("act", lambda e: e.activation(out=LR1[64:128, c0:c0 + n], in_=xs0[64:128, 0:n], func=AF.Copy), reads=[Bxs0], writes=[Bl])
                        else:
                            op("act", lambda e: e.activation(out=LR2[:, c0:c0 + n], in_=xs0[:, 0:n], func=AF.Sigmoid), reads=[Bxs0], writes=[Bl])
                        yield

                streams = [m1_stream(c) for c in range(4)] + [m2_stream(12), m2_stream(13)]
                alive = list(streams)
                while alive:
                    for g in list(alive):
                        try:
                            next(g)
                        except StopIteration:
                            alive.remove(g)
                S.dma("sp", conv_out[l].rearrange("(c p) g j -> p c g j", p=128), conv_o[:], "o1", reads=[Bout])
                S.dma("sp", h_out[l].rearrange("(c p) g -> p c g", p=128), hO[:], "o1", reads=[Bout])
            R["ck"](5)

            with Scope():
                prp = S.sb([128, SW]); Bprp = S.buf()
                dtm = S.sb([128, 256]); Bdt = S.buf()
                carry1 = S.sb([128, 14]); Bc1 = S.buf()
                op("pool", lambda e: e.memset(carry1[:], 0.0), writes=[Bc1])

                def proj_shift(jch, ws, Bws, ti, c0, n, out, Bo):
                    pb, Bp = proj(ws, Bws, c0, n)
                    if ti != SMP:
                        op("act", lambda e: e.activation(out=prp[:, 1:1 + n], in_=pb[:, 0:n], func=AF.Copy), reads=[Bp], writes=[Bprp])
                        op("dve", lambda e: e.tensor_copy(out=prp[:, 0:1], in_=carry1[:, jch:jch + 1]), reads=[Bc1], writes=[Bprp])
                        op("dve", lambda e: e.tensor_copy(out=carry1[:, jch:jch + 1], in_=prp[:, n:n + 1]), reads=[Bprp], writes=[Bc1])
                        if ti == LASTP:
                            op("pool", lambda e: e.tensor_copy(out=shO[:, jch, 0:1], in_=prp[:, n:n + 1]), reads=[Bprp], writes=[Bout])
                        cur, prev, dv, ov = prp[:, 1:1 + n], prp[:, 0:n], dtm[:, 0:n], out[:, 0:n]
                    else:
                        p3 = prp[:, 0:NB * 5].rearrange("p (b j) -> p b j", j=5)
                        op("act", lambda e: e.activation(out=p3[:, :, 1:5], in_=pb[:, 0:n].rearrange("p (b t) -> p b t", t=4), func=AF.Copy), reads=[Bp], writes=[Bprp])
                        op("dve", lambda e: e.tensor_copy(out=p3[:, :, 0], in_=shI[:, jch, :]), reads=[Bin], writes=[Bprp])
                        op("pool", lambda e: e.tensor_copy(out=shO[:, jch, 1:NG], in_=p3[:, :, 4]), reads=[Bprp], writes=[Bout])
                        cur, prev = p3[:, :, 1:5], p3[:, :, 0:4]
                        dv = dtm[:, 0:n].rearrange("p (b t) -> p b t", t=4)
                        ov = out[:, 0:n].rearrange("p (b t) -> p b t", t=4)
                    op("dve", lambda e: e.tensor_tensor(out=dv, in0=prev, in1=cur, op=ALU.subtract), reads=[Bprp], writes=[Bdt])
                    op("dve", lambda e: e.scalar_tensor_tensor(out=ov, in0=dv, scalar=muv[:, jch:jch + 1], in1=cur, op0=ALU.mult, op1=ALU.add), reads=[Bdt, Bprp, Bpv], writes=[Bo])

                R["ck"](6)
                names = ["r", "k", "v", "lw", "cum", "ecp", "ecn", "ar", "kk", "g", "bon", "t1"]
                Wsets = [{k: (S.sb([128, 256]), S.buf()) for k in names} for _ in range(2)]
                WBsets = [{k: (S.sb([128, 256], BF16), S.buf()) for k in ["RT", "KT", "AT", "BT", "V"]} for _ in range(2)]
                yfm_t = (S.sb([128, 256]), S.buf())
                ST = S.sb([128, 64]); STb = S.sb([128, 64], BF16); BST = S.buf()
                STs = S.sb([128, NB, 64]); STsb = S.sb([128, NB, 64], BF16); BSTs = S.buf()
                identb = S.sb([128, 128], BF16); Bidb = S.buf()
                op("act", lambda e: e.activation(out=identb[:], in_=ident, func=AF.Copy), reads=[Bc], writes=[Bidb])
                TM = S.sb([128, 4 * 384], BF16); BTM = S.buf()
                SCt = [(None if i_ == 1 else S.sb([128, 512], BF16)) for i_ in range(4)]; BSC = [S.buf() for _ in range(4)]
                PPt = [[S.sb([128, 512]) for _ in range(2)] for _ in range(2)]; BPP = [[S.buf() for _ in range(2)] for _ in range(2)]
                Tt = [[S.sb([128, 256]) for _ in range(2)] for _ in range(2)]; BT_ = [[S.buf() for _ in range(2)] for _ in range(2)]
                U0t = S.sb([128, 512]); BU0 = S.buf()
                WFsets = [{k: (S.sb([128, 256]), S.buf()) for k in ["ATf", "BTf"]} for _ in range(2)]
                UTt = S.sb([128, 512], BF16); BUT = S.buf()
                ytm = S.sb([128, 16, 128]); Bytm = S.buf()
                gst = S.sb([128, 32, 4]); Bgst = S.buf()
                S_sq = S.sb([128, 512]); Bsqs = S.buf()
                nrv = S.sb([128, 7, 4])
                op("dve", lambda e: e.tensor_scalar(out=nrv[:], in0=rv[:], scalar1=-1.0, scalar2=None, op0=ALU.mult), reads=[Bpv], writes=[Bpv])

                if os.environ.get("MK_MEM"):
                    print("M3 scope SBUF bytes remaining:", nc.sbuf_bytes_remaining)

                def wkv_group(chunks, C, units_list, ci0, st):
                    nch = len(chunks); nq = 2 * nch
                    WB = WBsets[st]; WF = WFsets[st]
                    fill = R["fill"]
                    RT, KT, AT, BT, V = (WB[k][0] for k in ["RT", "KT", "AT", "BT", "V"])
                    BRT, BKT, BAT, BBT, BV = (WB[k][1] for k in ["RT", "KT", "AT", "BT", "V"])

                    def v3(t):
                        return t[0:C, 0:nq * C].rearrange("p (q c) -> p q c", c=C)

                    def bcm(mk):
                        return mk[0:C, 0:C].unsqueeze(1).broadcast_to([C, nq, C])
                    TMv = TM[0:C, 0:nch * 384].rearrange("p (h i f) -> p h i f", i=3, f=128)
                    for h0 in range(0, nch, 2):
                        hn = min(2, nch - h0)
                        for hh in range(hn):
                            tp, Btp = pbank()
                            tpb = tp[0:C, 0:384].rearrange("p (i f) -> p i f", f=128)
                            c0, c1 = chunks[h0 + hh]
                            for i, (src, Bsrc) in enumerate([(KT, BKT), (BT, BBT), (V, BV)]):
                                op("pe", lambda e: e.matmul(tpb[:, i, :], lhsT=src[:, c0:c1], rhs=identb[:], start=True, stop=True), reads=[Bsrc, Bidb], writes=[Btp])
                            op("act", lambda e: e.activation(out=TMv[:, h0 + hh], in_=tpb, func=AF.Copy), reads=[Btp], writes=[BTM])
                    R["ck"](6.9)
                    ATf, BATf = WF["ATf"]; BTf, BBTf = WF["BTf"]
                    pairs = [(KT, AT, BKT, BAT), (BTf, ATf, BBTf, BATf), (KT, RT, BKT, BRT), (BT, RT, BBT, BRT), (ATf, BTf, BATf, BBTf)]
                    masks = [cst[:, 2, :], cst[:, 2, :], cst[:, 4, :], cst[:, 4, :], cst[:, 6, :]]

                    def ppv(j, pp):
                        return PPt[j][pp][0:C, 0:2 * nch * C].rearrange("p (t h c) -> p t h c", t=2, c=C)

                    def tv(j, pp):
                        return Tt[j][pp][0:C, 0:nch * C].rearrange("p (h c) -> p h c", c=C)
                    for i in range(5):
                        lh, rh, Bl, Br = pairs[i]
                        for j in range(2):
                            pj = slice(64 * j, 64 * j + 64)
                            sb_, Bsb = pbank()
                            sbv = sb_[0:C, 0:nch * C].rearrange("p (h c) -> p h c", c=C)
                            for ch in range(nch):
                                c0, c1 = chunks[ch]
                                op("pe", lambda e: e.matmul(sbv[:, ch, :], lhsT=lh[pj, c0:c1], rhs=rh[pj, c0:c1], start=True, stop=True), reads=[Bl, Br], writes=[Bsb])
                            if i == 1:
                                dv, Bd = ppv(j, 0)[:, 1], BPP[j][0]
                            elif i == 4:
                                dv, Bd = ppv(j, 0)[:, 0], BPP[j][0]
                            else:
                                dv, Bd = SCt[i][0:C, 0:nq * C].rearrange("p (h j c) -> p h j c", j=2, c=C)[:, :, j, :], BSC[i]
                            op("dve", lambda e: e.tensor_tensor(out=dv, in0=sbv, in1=masks[i][0:C, 0:C].unsqueeze(1).broadcast_to([C, nch, C]), op=ALU.mult), reads=[Bsb, Bc], writes=[Bd])
                            fill()
                    R["ck"](7.1)
                    m = 1
                    while (1 << m) < C:
                        m += 1
                    for j in range(2):
                        op("dve", lambda e: e.tensor_tensor(out=tv(j, 0), in0=ppv(j, 0)[:, 1], in1=ident[0:C, 0:C].unsqueeze(1).broadcast_to([C, nch, C]), op=ALU.add),
                           reads=[BPP[j][0], Bc], writes=[BT_[j][0]])
                    tcur = 0
                    for i in range(1, m):
                        cur, nxt = (i - 1) % 2, i % 2
                        last = (i == m - 1)
                        bk = []
                        for j in range(2):
                            pa, Bpa = pbank()
                            pav = pa[0:C, 0:2 * nch * C].rearrange("p (t h c) -> p t h c", t=2, c=C)
                            for ch in range(nch):
                                op("pe", lambda e: e.matmul(pav[:, 0, ch, :], lhsT=ppv(j, cur)[:, 1, ch, :], rhs=ppv(j, cur)[:, 0, ch, :], start=True, stop=True), reads=[BPP[j][cur]], writes=[Bpa])
                            if not last:
                                for ch in range(nch):
                                    op("pe", lambda e: e.matmul(pav[:, 1, ch, :], lhsT=ppv(j, cur)[:, 0, ch, :], rhs=ppv(j, cur)[:, 1, ch, :], start=True, stop=True), reads=[BPP[j][cur]], writes=[Bpa])
                            bk.append((pav, Bpa))
                        for j in range(2):
                            pav, Bpa = bk[j]
                            nt_ = 1 if last else 2
                            if j == 0:
                                op("act", lambda e: e.activation(out=ppv(j, nxt)[:, 0:nt_], in_=pav[:, 0:nt_], func=AF.Copy), reads=[Bpa], writes=[BPP[j][nxt]])
                            else:
                                op("dve", lambda e: e.tensor_copy(out=ppv(j, nxt)[:, 0:nt_], in_=pav[:, 0:nt_]), reads=[Bpa], writes=[BPP[j][nxt]])
                        for j in range(2):
                            pt_, Bpt_ = pbank()
                            ptv = pt_[0:C, 0:nch * C].rearrange("p (h c) -> p h c", c=C)
                            for ch in range(nch):
                                op("pe", lambda e: e.matmul(ptv[:, ch, :], lhsT=ppv(j, nxt)[:, 0, ch, :], rhs=tv(j, tcur)[:, ch, :], start=True, stop=True), reads=[BPP[j][nxt], BT_[j][tcur]], writes=[Bpt_])
                            op("dve", lambda e: e.tensor_tensor(out=tv(j, 1 - tcur), in0=ptv, in1=tv(j, tcur), op=ALU.add), reads=[Bpt_, BT_[j][tcur]], writes=[BT_[j][1 - tcur]])
                        fill(); fill(); fill()
                        tcur = 1 - tcur
                    R["ck"](7.2)
                    for units in units_list:
                        nu = len(units)
                        pu0, Bpu0 = pbank()
                        for u, (ch, Sf, Sb, BS, wc) in enumerate(units):
                            c0, c1 = chunks[ch]
                            for j in range(2):
                                pj = slice(64 * j, 64 * j + 64)
                                oc = slice((u * 2 + j) * 64, (u * 2 + j) * 64 + 64)
                                q = ch * 2 + j
                                op("pe", lambda e: e.matmul(pu0[0:C, oc], lhsT=AT[pj, c0:c1], rhs=Sb[pj, :], start=True, stop=False), reads=[BAT, BS], writes=[Bpu0])
                                op("pe", lambda e: e.matmul(pu0[0:C, oc], lhsT=v3(SCt[0])[:, q, :], rhs=TMv[:, ch, 2, pj], start=False, stop=True), reads=[BSC[0], BTM], writes=[Bpu0])
                        op("act", lambda e: e.activation(out=U0t[0:C, 0:nu * 128], in_=pu0[0:C, 0:nu * 128], func=AF.Copy), reads=[Bpu0], writes=[BU0])
                        pu, Bpu = pbank()
                        for u, (ch, Sf, Sb, BS, wc) in enumerate(units):
                            for j in range(2):
                                oc = slice((u * 2 + j) * 64, (u * 2 + j) * 64 + 64)
                                q = ch * 2 + j
                                op("pe", lambda e: e.matmul(pu[0:C, oc], lhsT=tv(j, tcur)[:, ch, :], rhs=U0t[0:C, oc], start=True, stop=True), reads=[BT_[j][tcur], BU0], writes=[Bpu])
                        op("act", lambda e: e.activation(out=UTt[0:C, 0:nu * 128], in_=pu[0:C, 0:nu * 128], func=AF.Copy), reads=[Bpu], writes=[BUT])
                        yp, Byp = pbank()
                        for u, (ch, Sf, Sb, BS, wc) in enumerate(units):
                            c0, c1 = chunks[ch]
                            for j in range(2):
                                pj = slice(64 * j, 64 * j + 64)
                                oc = slice((u * 2 + j) * 64, (u * 2 + j) * 64 + 64)
                                q = ch * 2 + j
                                op("pe", lambda e: e.matmul(yp[0:C, oc], lhsT=RT[pj, c0:c1], rhs=Sb[pj, :], start=True, stop=False), reads=[BRT, BS], writes=[Byp])
                                op("pe", lambda e: e.matmul(yp[0:C, oc], lhsT=v3(SCt[2])[:, q, :], rhs=TMv[:, ch, 2, pj], start=False, stop=False), reads=[BSC[2], BTM], writes=[Byp])
                                op("pe", lambda e: e.matmul(yp[0:C, oc], lhsT=v3(SCt[3])[:, q, :], rhs=UTt[0:C, oc], start=False, stop=True), reads=[BSC[3], BUT], writes=[Byp])
                        ch0 = units[0][0]
                        op("act", lambda e: e.activation(out=ytm[0:C, ci0 + ch0:ci0 + ch0 + nu, :], in_=yp[0:C, 0:nu * 128].rearrange("p (u f) -> p u f", f=128), func=AF.Copy),
                           reads=[Byp], writes=[Bytm])
                        sn, Bsn = pbank()
                        for u, (ch, Sf, Sb, BS, wc) in enumerate(units):
                            ocu = slice(u * 128, u * 128 + 128)
                            op("pe", lambda e: e.matmul(sn[:, ocu], lhsT=TMv[:, ch, 0, :], rhs=TMv[:, ch, 2, :], start=True, stop=False), reads=[BTM], writes=[Bsn])
                            op("pe", lambda e: e.matmul(sn[:, ocu], lhsT=TMv[:, ch, 1, :], rhs=UTt[0:C, ocu], start=False, stop=True), reads=[BTM, BUT], writes=[Bsn])
                        R["st_update"](units, sn, Bsn)
                        fill(); fill()
                        R["ck"](7.3)

                def st_update_prompt(units, sn, Bsn):
                    (ch, Sf, Sb, BS, wc) = units[0]
                    for j in range(2):
                        pj = slice(64 * j, 64 * j + 64)
                        op("dve", lambda e: e.tensor_tensor(out=Sf[pj, :], in0=sn[pj, pj], in1=Sf[pj, :], op=ALU.add), reads=[Bsn, BS], writes=[BS])
                    op("dve", lambda e: e.tensor_scalar(out=Sf, in0=Sf, scalar1=wc, scalar2=None, op0=ALU.mult), reads=[BS, Wsets[R["set"]]["ecp"][1]], writes=[BS])
                    op("act", lambda e: e.activation(out=Sb, in_=Sf, func=AF.Copy), reads=[BS], writes=[BS])

                def st_update_sample(units, sn, Bsn):
                    b0 = units[0][0] + R["b_base"]
                    nu = len(units)
                    Sf = STs[:, b0:b0 + nu, :]; Sb = STsb[:, b0:b0 + nu, :]
                    W = Wsets[R["set"]]
                    ecp = W["ecp"][0]
                    wcs = ecp[:, 0:NS].rearrange("p (b t) -> p b t", t=4)[:, b0:b0 + nu, 3]
                    for j in range(2):
                        pj = slice(64 * j, 64 * j + 64)
                        op("dve", lambda e: e.tensor_tensor(out=STs[pj, b0:b0 + nu, :], in0=sn[pj, 0:nu * 128].rearrange("p (u f) -> p u f", f=128)[:, :, pj], in1=STs[pj, b0:b0 + nu, :], op=ALU.add),
                           reads=[Bsn, BSTs], writes=[BSTs])
                    op("dve", lambda e: e.tensor_tensor(out=Sf, in0=Sf, in1=wcs.unsqueeze(2).broadcast_to([128, nu, 64]), op=ALU.mult), reads=[BSTs, W["ecp"][1]], writes=[BSTs])
                    op("act", lambda e: e.activation(out=Sb, in_=Sf, func=AF.Copy), reads=[BSTs], writes=[BSTs])

                def groupnorm(C, g0, ng_):
                    y4 = ytm[0:C, :, :].rearrange("p c (j v) -> p (c j) v", v=64)[:, g0:g0 + ng_, :]
                    gs = gst[0:C, 0:ng_, :]
                    op("dve", lambda e: e.tensor_reduce(out=gs[:, :, 0], in_=y4, axis=AX.X, op=ALU.add), reads=[Bytm], writes=[Bgst])
                    op("dve", lambda e: e.tensor_scalar(out=gs[:, :, 0], in0=gs[:, :, 0], scalar1=1.0 / 64, scalar2=None, op0=ALU.mult), reads=[Bgst], writes=[Bgst])
                    op("dve", lambda e: e.tensor_tensor(out=y4, in0=y4, in1=gs[:, :, 0:1].broadcast_to([C, ng_, 64]), op=ALU.subtract), reads=[Bytm, Bgst], writes=[Bytm])
                    sq_t = S_sq[0:C, 0:ng_ * 64].rearrange("p (g v) -> p g v", v=64)
                    op("pool", lambda e: e.tensor_tensor(out=sq_t, in0=y4, in1=y4, op=ALU.mult), reads=[Bytm], writes=[Bsqs])
                    op("dve", lambda e: e.tensor_reduce(out=gs[:, :, 1], in_=sq_t, axis=AX.X, op=ALU.add), reads=[Bsqs], writes=[Bgst])
                    op("act", lambda e: e.activation(out=gs[:, :, 1], in_=gs[:, :, 1], func=AF.Ln, bias=eps_t[0:C, 2:3], scale=1.0 / 64), reads=[Bgst, Bsm], writes=[Bgst])
                    op("act", lambda e: e.activation(out=gs[:, :, 1], in_=gs[:, :, 1], func=AF.Exp, scale=-0.5), reads=[Bgst], writes=[Bgst])
                    op("dve", lambda e: e.tensor_tensor(out=y4, in0=y4, in1=gs[:, :, 1:2].broadcast_to([C, ng_, 64]), op=ALU.mult), reads=[Bytm, Bgst], writes=[Bytm])

                def make_hp(hp):
                    wsl = []
                    rvv = lambda q, hp=hp: rv[:, q, hp:hp + 1]
                    hc = slice(hp * 128, hp * 128 + 128)

                    def load():
                        wsl.extend(load_w(wslice(w_in[l], 1024 + (q * 4 + hp) * 128), 8) for q in range(3))

                    def init():
                        op("dve", lambda e: e.memset(ST[:], 0.0), writes=[BST])
                        op("dve", lambda e: e.memset(STb[:], 0.0), writes=[BST])
                        S.dma("sp", STs[:], wkv_st[l, :, 2 * hp:2 * hp + 2].rearrange("b j k v -> (j k) b v"), "sts", writes=[BSTs])
                        op("act", lambda e: e.activation(out=STsb[:], in_=STs[:], func=AF.Copy), reads=[BSTs], writes=[BSTs])

                    def pre(ti, st, hp=hp, wsl=wsl, rvv=rvv, hc=hc):
                        c0, n = MT[ti]
                        smp = ti == SMP
                        W = Wsets[st]; WB = WBsets[st]; WF = WFsets[st]
                        for q, nm in enumerate(["r", "k", "v"]):
                            proj_shift(q * 4 + hp, wsl[q][0], wsl[q][1], ti, c0, n, W[nm][0], W[nm][1])
                            yield
                        r_, Br = W["r"]; k_, Bk = W["k"]; v_, Bv = W["v"]; lw, Blw = W["lw"]; cum, Bcum = W["cum"]
                        ecp, Becp = W["ecp"]; ecn, Becn = W["ecn"]; ar, Bar = W["ar"]; kk, Bkk = W["kk"]
                        g_, Bg = W["g"]; bon, Bbon = W["bon"]
                        RTb, BRT = WB["RT"]; KTb, BKT = WB["KT"]; ATb, BAT = WB["AT"]; BTb, BBT = WB["BT"]; Vb, BVb = WB["V"]
                        rmv = rmask[:, 256:320] if smp else rmask[:, 0:256]
                        pw, Bpw = pbank()
                        op("pe", lambda e: e.matmul(pw[:, 0:n], lhsT=w2a2b[0:64, hc], rhs=LR1[0:64, c0:c0 + n], start=True, stop=True), reads=[Blr, BLRs[0]], writes=[Bpw])
                        pa_, Bpa_ = pbank()
                        op("pe", lambda e: e.matmul(pa_[:, 0:n], lhsT=w2a2b[64:128, hc], rhs=LR1[64:128, c0:c0 + n], start=True, stop=True), reads=[Blr, BLRs[0]], writes=[Bpa_])
                        yield
                        pg2, Bpg2 = pbank()
                        op("pe", lambda e: e.matmul(pg2[:, 0:n], lhsT=g2b[:, hc], rhs=LR2[:, c0:c0 + n], start=True, stop=True), reads=[Blr, BLRs[1]], writes=[Bpg2])
                        op("act", lambda e: e.activation(out=lw[:, 0:n], in_=pw[:, 0:n], func=AF.Sigmoid, bias=rvv(0)), reads=[Bpw, Bpv], writes=[Blw])
                        op("act", lambda e: e.activation(out=ar[:, 0:n], in_=pa_[:, 0:n], func=AF.Sigmoid, bias=rvv(1)), reads=[Bpa_, Bpv], writes=[Bar])
                        yield
                        op("act", lambda e: e.activation(out=g_[:, 0:n], in_=pg2[:, 0:n], func=AF.Copy), reads=[Bpg2], writes=[Bg])
                        op("dve", lambda e: e.tensor_scalar(out=lw[:, 0:n], in0=lw[:, 0:n], scalar1=-0.6065306597126334, scalar2=None, op0=ALU.mult), reads=[Blw], writes=[Blw])
                        op("dve", lambda e: e.tensor_tensor_scan(out=cum[:, 0:n], data0=rmv, data1=lw[:, 0:n], initial=0.0, op0=ALU.mult, op1=ALU.add),
                           reads=[Brm, Blw], writes=[Bcum])
                        yield
                        op("dve", lambda e: e.tensor_tensor(out=lw[:, 0:n], in0=cum[:, 0:n], in1=lw[:, 0:n], op=ALU.subtract), reads=[Bcum, Blw], writes=[Blw])
                        t1, Bt1 = W["t1"]
                        op("dve", lambda e: e.tensor_scalar(out=kk[:, 0:n], in0=k_[:, 0:n], scalar1=rvv(2), scalar2=None, op0=ALU.mult), reads=[Bk, Bpv], writes=[Bkk])
                        op("pool", lambda e: e.tensor_tensor(out=t1[:, 0:n], in0=kk[:, 0:n], in1=kk[:, 0:n], op=ALU.mult), reads=[Bkk], writes=[Bt1])
                        yield
                        pss, Bpss = pbank()
                        op("pe", lambda e: e.matmul(pss[:, 0:n], lhsT=blk1, rhs=t1[:, 0:n], start=True, stop=True), reads=[Bc, Bt1], writes=[Bpss])
                        op("act", lambda e: e.activation(out=ecp[:, 0:n], in_=cum[:, 0:n], func=AF.Exp), reads=[Bcum], writes=[Becp])
                        op("act", lambda e: e.activation(out=ecn[:, 0:n], in_=cum[:, 0:n], func=AF.Exp, scale=-1.0), reads=[Bcum], writes=[Becn])
                        yield
                        op("act", lambda e: e.activation(out=lw[:, 0:n], in_=lw[:, 0:n], func=AF.Exp), reads=[Blw], writes=[Blw])
                        op("dve", lambda e: e.tensor_scalar(out=t1[:, 0:n], in0=pss[:, 0:n], scalar1=1e-18, scalar2=None, op0=ALU.max), reads=[Bpss], writes=[Bt1])
                        op("act", lambda e: e.activation(out=t1[:, 0:n], in_=t1[:, 0:n], func=AF.Ln), reads=[Bt1], writes=[Bt1])
                        yield
                        op("act", lambda e: e.activation(out=t1[:, 0:n], in_=t1[:, 0:n], func=AF.Exp, scale=-0.5), reads=[Bt1], writes=[Bt1])
                        op("dve", lambda e: e.tensor_tensor(out=kk[:, 0:n], in0=kk[:, 0:n], in1=t1[:, 0:n], op=ALU.mult), reads=[Bkk, Bt1], writes=[Bkk])
                        op("dve", lambda e: e.tensor_scalar(out=t1[:, 0:n], in0=ar[:, 0:n], scalar1=-1.0, scalar2=rvv(3), op0=ALU.add, op1=ALU.mult), reads=[Bar, Bpv], writes=[Bt1])
                        yield
                        op("dve", lambda e: e.scalar_tensor_tensor(out=k_[:, 0:n], in0=t1[:, 0:n], scalar=1.0, in1=k_[:, 0:n], op0=ALU.add, op1=ALU.mult), reads=[Bt1, Bk], writes=[Bk])
                        op("dve", lambda e: e.scalar_tensor_tensor(out=t1[:, 0:n], in0=r_[:, 0:n], scalar=rvv(4), in1=k_[:, 0:n], op0=ALU.mult, op1=ALU.mult), reads=[Br, Bk, Bpv], writes=[Bt1])
                        pbs, Bpbs = pbank()
                        op("pe", lambda e: e.matmul(pbs[:, 0:n], lhsT=blk1, rhs=t1[:, 0:n], start=True, stop=True), reads=[Bc, Bt1], writes=[Bpbs])
                        yield
                        op("dve", lambda e: e.tensor_tensor(out=bon[:, 0:n], in0=pbs[:, 0:n], in1=v_[:, 0:n], op=ALU.mult), reads=[Bpbs, Bv], writes=[Bbon])
                        op("pool", lambda e: e.tensor_tensor(out=RTb[:, 0:n], in0=r_[:, 0:n], in1=ecp[:, 0:n], op=ALU.mult), reads=[Br, Becp], writes=[BRT])
                        op("pool", lambda e: e.tensor_tensor(out=KTb[:, 0:n], in0=k_[:, 0:n], in1=ecn[:, 0:n], op=ALU.mult), reads=[Bk, Becn], writes=[BKT])
                        yield
                        op("pool", lambda e: e.tensor_copy(out=Vb[:, 0:n], in_=v_[:, 0:n]), reads=[Bv], writes=[BVb])
                        ATf, BATf = WF["ATf"]; BTf, BBTf = WF["BTf"]
                        op("dve", lambda e: e.tensor_tensor(out=t1[:, 0:n], in0=kk[:, 0:n], in1=ar[:, 0:n], op=ALU.mult), reads=[Bkk, Bar], writes=[Bt1])
                        op("dve", lambda e: e.tensor_tensor(out=BTf[:, 0:n], in0=t1[:, 0:n], in1=ecn[:, 0:n], op=ALU.mult), reads=[Bt1, Becn], writes=[BBTf])
                        yield
                        op("dve", lambda e: e.scalar_tensor_tensor(out=ATf[:, 0:n], in0=kk[:, 0:n], scalar=-1.0, in1=lw[:, 0:n], op0=ALU.mult, op1=ALU.mult), reads=[Bkk, Blw], writes=[BATf])
                        op("pool", lambda e: e.tensor_copy(out=BTb[:, 0:n], in_=BTf[:, 0:n]), reads=[BBTf], writes=[BBT])
                        op("pool", lambda e: e.tensor_copy(out=ATb[:, 0:n], in_=ATf[:, 0:n]), reads=[BATf], writes=[BAT])
                        yield

                    def core(ti, st, hp=hp, rvv=rvv):
                        c0, n = MT[ti]
                        smp = ti == SMP
                        W = Wsets[st]
                        R["set"] = st
                        ecp, Becp = W["ecp"]; g_, Bg = W["g"]; bon, Bbon = W["bon"]
                        yfm, Byfm = yfm_t
                        R["ck"](6.7)
                        if not smp:
                            nci, C = n // CH, CH
                            chunks = [(ci * CH, ci * CH + CH) for ci in range(nci)]
                            R["st_update"] = st_update_prompt
                            units_list = [[(ci, ST[:, :], STb[:, :], BST, ecp[:, ci * CH + CH - 1:ci * CH + CH])] for ci in range(nci)]
                            wkv_group(chunks, CH, units_list, 0, st)
                            if ti == LASTP:
                                S.dma("sp", wkv_out[l, 0, 2 * hp:2 * hp + 2].rearrange("j k v -> (j k) v"), ST[:], "o2", reads=[BST])
                            groupnorm(C, 0, nci * 2)
                        else:
                            nci, C = NB, 4
                            R["st_update"] = st_update_sample
                            for g4 in range(4):
                                chunks = [((g4 * 4 + b) * 4, (g4 * 4 + b) * 4 + 4) for b in range(4)]
                                R["b_base"] = g4 * 4
                                units = [(b, STs[:, g4 * 4 + b, :], STsb[:, g4 * 4 + b, :], BSTs, None) for b in range(4)]
                                wkv_group(chunks, 4, [units], g4 * 4, st)
                            S.dma("sp", wkv_out[l, 1:NG, 2 * hp:2 * hp + 2].rearrange("b j k v -> (j k) b v"), STs[:], "o2", reads=[BSTs])
                            for q4 in range(4):
                                groupnorm(C, q4 * 8, 8)
                        pyt, Bpyt = pbank()
                        for ci in range(nci):
                            op("pe", lambda e, ci=ci: e.transpose(pyt[:, ci * C:ci * C + C], ytm[0:C, ci, :], ident[0:C, 0:C]), reads=[Bytm, Bc], writes=[Bpyt])
                        op("act", lambda e: e.activation(out=yfm[:, 0:n], in_=pyt[:, 0:n], func=AF.Identity, scale=rvv(5), bias=rvv(6)), reads=[Bpyt, Bpv], writes=[Byfm])
                        op("dve", lambda e: e.tensor_tensor(out=yfm[:, 0:n], in0=yfm[:, 0:n], in1=bon[:, 0:n], op=ALU.add), reads=[Byfm, Bbon], writes=[Byfm])
                        op("dve", lambda e: e.tensor_tensor(out=ymix[:, 4 + hp, c0:c0 + n], in0=yfm[:, 0:n], in1=g_[:, 0:n], op=ALU.mult), reads=[Byfm, Bg], writes=[Bbig])

                    return load, init, pre, core

                def drain(g):
                    for _ in g:
                        pass
                stages = [(hp_, ti_) for hp_ in range(4) for ti_ in range(len(MT))]
                ctx = {0: make_hp(0)}
                ctx[0][0]()
                R["fill"] = lambda: None
                drain(ctx[0][2](0, 0))
                for k_, (hp_, ti_) in enumerate(stages):
                    if ti_ == 0:
                        ctx[hp_][1]()
                    gen = iter(())
                    if k_ + 1 < len(stages):
                        nh, nt_i = stages[k_ + 1]
                        if nt_i == 0:
                            ctx[nh] = make_hp(nh)
                            ctx[nh][0]()
                        gen = ctx[nh][2](nt_i, (k_ + 1) % 2)
                    R["fill"] = lambda gen=gen: next(gen, None)
                    ctx[hp_][3](ti_, k_ % 2)
                    drain(gen)
                    R["fill"] = lambda: None
                S.dma("sp", shift_out[l].rearrange("(c p) g -> p c g", p=128), shO[:], "o1", reads=[Bout])

        def out_proj(l):
            G2 = mod[l][:, 40:48, :]
            pend = [load_w(wslice(w_out[l], 0), 8)]
            for i in range(8):
                wo, Bwo = pend.pop(0)
                if i + 1 < 8:
                    pend.append(load_w(wslice(w_out[l], (i + 1) * 128), 8))
                for ti, (c0, n) in enumerate(TT):
                    pb, Bp = pbank()
                    for kc in range(8):
                        op("pe", lambda e, kc=kc, pb=pb, wo=wo, c0=c0, n=n: e.matmul(pb[:, 0:n], lhsT=wo[:, kc, :], rhs=ymix[:, kc, c0:c0 + n], start=(kc == 0), stop=(kc == 7)),
                           reads=[Bwo, Bbig], writes=[Bp])
                    resid_add(pb, Bp, i, c0, n, G2, Bmod[l][1])

        def open_x(src):
            R["x"] = S.sb([128, 8, NT]); R["ptmp"] = S.sb([128, 512]); R["Bpt"] = S.buf()
            for i in range(8):
                S.dma("sp", R["x"][:, i, :], src[i * 128:(i + 1) * 128, :], f"x{i}", writes=[Bx[i]])

        STOP = float(os.environ.get("MK_STOP", "99"))

        class Stop(Exception):
            pass

        def ck(k):
            if STOP <= k:
                raise Stop()
        R["ck"] = ck
        cur = {}

        def trunk():
          cur["sc"] = Scope(); cur["sc"].__enter__()
          open_x(xT)
          ck(1)
          for l in range(L):
            pf = ffn_prefetch(l, 0)
            norm_mod(mod[l][:, 8:16, :], mod[l][:, 0:8, :], Bmod[l][0], h_out_fn)
            if os.environ.get("MK_DUMPH") and STOP <= 2:
                for i in range(8):
                    op("dve", lambda e: e.tensor_copy(out=R["x"][:, i, :], in_=hT[:, i, :]), reads=[Bh, Bx[i]], writes=[Bx[i]])
            ck(2)
            if l == 0:
                ffn_with_stream(l, 0, mod[l][:, 16:24, :], pf, [(0, j) for j in range(24, 72)], [(0, 1), (0, 2)])
            else:
                ffn(l, 0, mod[l][:, 16:24, :], first=pf)
            ck(3)
            for i in range(8):
                S.dma("sp", xscr[i * 128:(i + 1) * 128, :], R["x"][:, i, :], f"x{i}", reads=[Bx[i]])
            norm_mod(mod[l][:, 32:40, :], mod[l][:, 24:32, :], Bmod[l][1], h_out_fn)
            cur["sc"].__exit__(None, None, None); cur["sc"] = None
            ck(4)
            with Scope():
                mixer(l)
            ck(8)
            cur["sc"] = Scope(); cur["sc"].__enter__()
            open_x(xscr)
            out_proj(l)
            ck(9)
            pf = ffn_prefetch(l, 1)
            norm_mod(mod[l][:, 56:64, :], mod[l][:, 48:56, :], Bmod[l][2], h_out_fn)
            if l == 0:
                ffn_with_stream(l, 1, mod[l][:, 64:72, :], pf, [(1, j) for j in range(72)], [(1, 0), (1, 1), (1, 2)])
            else:
                ffn(l, 1, mod[l][:, 64:72, :], first=pf)
            ck(10)
          yo = [S.sb([128, 8, 256])] * 2; Byo = [S.buf()] * 2
          yTv = yT.rearrange("(k p) t -> p k t", p=128)
          norm_mod(fA_t[:, :, :], fsh_t[:, :, :], Bsm,
                   lambda ti, c0, n: (yo[ti % 2][:, :, 0:n], Byo[ti % 2]),
                   after=lambda ti, c0, n: S.dma("sp", yTv[:, :, c0:c0 + n], yo[ti % 2][:, :, 0:n], "yo", reads=[Byo[ti % 2]]), nbuf=1)

        try:
            trunk()
        except Stop:
            if os.environ.get("MK_DUMPX") and cur.get("sc") is not None:
                for i in range(8):
                    S.dma("sp", yT[i * 128:(i + 1) * 128, :], R["x"][:, i, :], f"x{i}", reads=[Bx[i]])
        if cur.get("sc") is not None:
            cur["sc"].__exit__(None, None, None)
        S.barrier()
        S.emit()
    return nc


_CACHE = {}


def _consts():
    c = np.zeros((128, 9, 128), np.float32)
    c[:, 0, :] = np.eye(128)
    c[0:64, 1, 0:64] = 1.0
    c[64:128, 1, 64:128] = 1.0
    su = np.triu(np.ones((128, 128), np.float32), 1)
    iu = np.triu(np.ones((128, 128), np.float32), 0)
    c[:, 2, :] = su
    c[:, 3, :] = su
    c[:, 4, :] = iu
    c[:, 5, :] = iu
    c[:, 6, :] = su.T
    c[:, 7, :] = 1.0
    rm = np.ones((128, 320), np.float32)
    rm[:, 0:256:CH] = 0.0
    rm[:, 256::4] = 0.0
    return c, rm


def kernel(**inp):
    f = lambda a: np.ascontiguousarray(np.asarray(a, dtype=np.float32))
    I = {k: f(v) for k, v in inp.items()}
    if "nc" not in _CACHE:
        _CACHE["nc"] = build_program()
    nc = _CACHE["nc"]
    consts, rmask = _consts()
    chunk4 = lambda v: f(v.reshape(L, 4, 128).transpose(0, 2, 1))
    shared = {
        "w_ada": I["w_ada"],
        "b_adaT": f(I["b_ada"].reshape(L, 72, 128).transpose(0, 2, 1)),
        "normsT": f(np.stack([I["ffn1_norm"], I["mix_norm"], I["ffn2_norm"]], 1).reshape(L, 3, 8, 128).transpose(0, 1, 3, 2)),
        "fnormT": f(I["final_norm"].reshape(8, 128).T),
        "ffn1_w_gate": I["ffn1_w_gate"], "ffn2_w_gate": I["ffn2_w_gate"],
        "ffn1_w_up": I["ffn1_w_up"], "ffn2_w_up": I["ffn2_w_up"],
        "ffn1_w_down": I["ffn1_w_down"], "ffn2_w_down": I["ffn2_w_down"],
        "w_in": I["w_in"], "w_out": I["w_out"],
        "mu_T": f(I["rwkv_mu"].reshape(L, 14, 128).transpose(0, 2, 1)),
        "w2a2": f(np.concatenate([I["rwkv_w2"], I["rwkv_a2"]], axis=1)),
        "g2": I["rwkv_g2"],
        "consts": consts, "rmask": rmask,
    }
    bd = np.zeros((L, 2, 4, 128, 128), np.float32)
    for q, nm in enumerate(["lru_wx", "lru_wa"]):
        for c in range(4):
            for j in range(2):
                bd[:, q, c, 64 * j:64 * j + 64, 64 * j:64 * j + 64] = I[nm][:, 2 * c + j]
    shared["bd_gate"] = bd
    lv = np.zeros((L, 128, 8, 4), np.float32)
    for j in range(4):
        lv[:, :, j, :] = chunk4(I["lru_conv_w"][:, j])
    for j, nm in enumerate(["lru_conv_b", "lru_bx", "lru_ba", "lru_lambda"]):
        lv[:, :, 4 + j, :] = chunk4(I[nm])
    shared["lru_vec"] = lv
    rvv = np.zeros((L, 128, 7, 4), np.float32)
    for j, nm in enumerate(["rwkv_w0", "rwkv_a0", "rwkv_k_k", "rwkv_k_a", "rwkv_r_k", "rwkv_ln_w", "rwkv_ln_b"]):
        rvv[:, :, j, :] = chunk4(I[nm].reshape(L, 512))
    shared["rw_vec"] = rvv

    in_maps = []
    for c in range(8):
        sb = slice(c * NB, (c + 1) * NB)
        xs = I["x_sample"][sb].reshape(NS, D)
        m = dict(shared)
        m["xT"] = f(np.concatenate([I["x_prompt"][c], xs], 0).T)
        m["cT"] = f(np.concatenate([I["c_prompt"][c:c + 1], I["c_sample"][sb]], 0).T)
        m["conv_st"] = f(I["state_lru_conv"][:, sb].transpose(0, 3, 1, 2))
        m["h_st"] = f(I["state_lru_h"][:, sb].transpose(0, 2, 1))
        m["shift_st"] = f(I["state_rwkv_shift"][:, sb].transpose(0, 2, 1))
        m["wkv_st"] = f(I["state_rwkv_wkv"][:, sb].transpose(0, 1, 2, 4, 3))
        in_maps.append(m)
    res = run_bass_kernel_spmd(nc, in_maps, core_ids=list(range(8))).results

    B = 8
    y_p = np.zeros((B, NP_, D), np.float32); y_s = np.zeros((B * NB, 4, D), np.float32)
    p_conv = np.zeros((L, B, 3, 512), np.float32); s_conv = np.zeros((L, B * NB, 3, 512), np.float32)
    p_h = np.zeros((L, B, 512), np.float32); s_h = np.zeros((L, B * NB, 512), np.float32)
    p_sh = np.zeros((L, B, 1792), np.float32); s_sh = np.zeros((L, B * NB, 1792), np.float32)
    p_wkv = np.zeros((L, B, 8, 64, 64), np.float32); s_wkv = np.zeros((L, B * NB, 8, 64, 64), np.float32)
    for c in range(8):
        r = res[c]
        sb = slice(c * NB, (c + 1) * NB)
        yt = np.asarray(r["yT"]).T
        y_p[c] = yt[:NP_]
        y_s[sb] = yt[NP_:].reshape(NB, 4, D)
        co = np.asarray(r["conv_out"]).transpose(0, 2, 3, 1)
        p_conv[:, c] = co[:, 0]; s_conv[:, sb] = co[:, 1:]
        ho = np.asarray(r["h_out"]).transpose(0, 2, 1)
        p_h[:, c] = ho[:, 0]; s_h[:, sb] = ho[:, 1:]
        so = np.asarray(r["shift_out"]).transpose(0, 2, 1)
        p_sh[:, c] = so[:, 0]; s_sh[:, sb] = so[:, 1:]
        wo = np.asarray(r["wkv_out"]).transpose(0, 1, 2, 4, 3)
        p_wkv[:, c] = wo[:, 0]; s_wkv[:, sb] = wo[:, 1:]
    return (y_p, y_s, p_conv, p_h, p_sh, p_wkv, s_conv, s_h, s_sh, s_wkv)
```

```python
import os
import types
import numpy as np
from contextlib import ExitStack
import concourse.bass as bass
import concourse.mybir as mybir
from concourse.bass_utils import run_bass_kernel_spmd

F32 = mybir.dt.float32
BF16 = mybir.dt.bfloat16
AF = mybir.ActivationFunctionType
ALU = mybir.AluOpType
AX = mybir.AxisListType

L = 2
D = 1024
NP_ = 2048
NB = 16
NS = 64
NT = NP_ + NS
NG = 17
DFF = 2816
TT = [(0, 512), (512, 512), (1024, 512), (1536, 512), (2048, 64)]
CH = 128
NORM_EPS = 1e-6
GN_EPS = 64e-5


def _freeze(fn):
    if fn.__closure__ is None:
        return fn
    cells = []
    for c in fn.__closure__:
        try:
            cells.append(types.CellType(c.cell_contents))
        except ValueError:
            cells.append(c)
    return types.FunctionType(fn.__code__, fn.__globals__, fn.__name__, fn.__defaults__, tuple(cells))


class Buf:
    __slots__ = ("name", "w", "r", "excl")

    def __init__(self, name, excl=False):
        self.name = name
        self.excl = excl
        self.w = None
        self.r = []


class Sched:
    ENG = ("pe", "act", "dve", "pool", "sp")

    def __init__(self, nc, stack, same_engine_sync=("pool",)):
        self.nc = nc
        self.stack = stack
        self.root = stack
        self.sems = {}
        self.cnt = {}
        for e in self.ENG:
            self.sems[e] = stack.enter_context(nc.semaphore("s_" + e))
            self.cnt[e] = 0
        self.known = {e: {} for e in self.ENG}
        self.prog = {e: [] for e in self.ENG}
        self.same = set(same_engine_sync)
        self.dma_sems = {}
        self.dma_cnt = {}
        self.nbuf = 0
        self.nt = 0

    def sb(self, shape, dt=F32, name=None):
        self.nt += 1
        return self.stack.enter_context(self.nc.sbuf_tensor(name or f"t{self.nt}", list(shape), dt))

    def ps(self, shape, dt=F32, name=None):
        self.nt += 1
        return self.stack.enter_context(self.nc.psum_tensor(name or f"p{self.nt}", list(shape), dt))

    def buf(self, name=None, excl=False):
        self.nbuf += 1
        return Buf(name or f"b{self.nbuf}", excl)

    def dma_sem(self, key):
        if key not in self.dma_sems:
            self.dma_sems[key] = self.root.enter_context(self.nc.semaphore("d_" + key))
            self.dma_cnt[key] = 0
        return self.dma_sems[key]

    def _need(self, eng, tag, waits):
        if tag is None:
            return
        key, val = tag
        if key == eng and eng not in self.same:
            return
        if self.known[eng].get(key, 0) >= val:
            return
        self.known[eng][key] = val
        waits[key] = max(waits.get(key, 0), val)

    def _deps(self, eng, reads, writes):
        waits = {}
        excl = [b for b in reads if b.excl]
        for b in reads:
            self._need(eng, b.w, waits)
        for b in list(writes) + excl:
            self._need(eng, b.w, waits)
            for t in b.r:
                self._need(eng, t, waits)
        return waits

    def _semobj(self, key):
        return self.sems[key] if key in self.sems else self.dma_sems[key]

    def _mark(self, tag, reads, writes):
        for b in reads:
            if b.excl:
                b.w = tag
                b.r = []
            else:
                b.r.append(tag)
        for b in writes:
            b.w = tag
            b.r = []

    def op(self, eng, fn, reads=(), writes=()):
        waits = self._deps(eng, reads, writes)
        self.cnt[eng] += 1
        tag = (eng, self.cnt[eng])
        fn = _freeze(fn)
        self.prog[eng].append((fn, [(self._semobj(k), v) for k, v in waits.items()], (self.sems[eng], 1)))
        self._mark(tag, reads, writes)
        for b in reads:
            if not b.excl and len(b.r) > 8:
                last = {}
                for k, v in b.r:
                    last[k] = max(last.get(k, 0), v)
                b.r = list(last.items())
        return tag

    def dma(self, q, out, in_, semkey, reads=(), writes=()):
        waits = self._deps(q, reads, writes)
        sem = self.dma_sem(semkey)
        if self.dma_cnt[semkey] > 0:
            self._need(q, (semkey, self.dma_cnt[semkey]), waits)
        self.dma_cnt[semkey] += 16
        tag = (semkey, self.dma_cnt[semkey])

        def fn(e, out=out, in_=in_):
            return e.dma_start(out=out, in_=in_)
        self.prog[q].append((fn, [(self._semobj(k), v) for k, v in waits.items()], (sem, 16)))
        self._mark(tag, reads, writes)
        return tag

    def barrier(self):
        for e in self.ENG:
            waits = {}
            for o in self.ENG:
                if o != e and self.cnt[o] > 0:
                    self._need(e, (o, self.cnt[o]), waits)
            for k, v in self.dma_cnt.items():
                if v > 0:
                    self._need(e, (k, v), waits)
            if waits:
                self.prog[e].append((None, [(self._semobj(k), v) for k, v in waits.items()], None))

    def emit(self):
        nc = self.nc
        with nc.Block() as block:
            def run(ename):
                def body(e):
                    for fn, waits, inc in self.prog[ename]:
                        for s, v in waits:
                            e.wait_ge(s, v)
                        if fn is None:
                            continue
                        ins = fn(e)
                        if inc is not None:
                            ins.then_inc(inc[0], inc[1])
                return body
            block.tensor(run("pe"))
            block.scalar(run("act"))
            block.vector(run("dve"))
            block.gpsimd(run("pool"))
            block.sync(run("sp"))


def build_program():
    nc = bass.Bass("TRN2", target_bir_lowering=False)

    def din(name, shape):
        return nc.dram_tensor(name, list(shape), F32, kind="ExternalInput").ap()

    def dout(name, shape):
        return nc.dram_tensor(name, list(shape), F32, kind="ExternalOutput").ap()

    xT = din("xT", [D, NT])
    cT = din("cT", [D, NG])
    conv_st = din("conv_st", [L, 512, NB, 3])
    h_st = din("h_st", [L, 512, NB])
    shift_st = din("shift_st", [L, 1792, NB])
    wkv_st = din("wkv_st", [L, NB, 8, 64, 64])
    w_ada = din("w_ada", [L, D, 9 * D])
    b_adaT = din("b_adaT", [L, 128, 72])
    normsT = din("normsT", [L, 3, 128, 8])
    fnormT = din("fnormT", [128, 8])
    wg_d = [din("ffn1_w_gate", [L, D, DFF]), din("ffn2_w_gate", [L, D, DFF])]
    wu_d = [din("ffn1_w_up", [L, D, DFF]), din("ffn2_w_up", [L, D, DFF])]
    wd_d = [din("ffn1_w_down", [L, DFF, D]), din("ffn2_w_down", [L, DFF, D])]
    w_in = din("w_in", [L, D, DFF])
    w_out = din("w_out", [L, D, D])
    bd_gate = din("bd_gate", [L, 2, 4, 128, 128])
    lru_vec = din("lru_vec", [L, 128, 8, 4])
    rw_vec = din("rw_vec", [L, 128, 7, 4])
    mu_T = din("mu_T", [L, 128, 14])
    w2a2 = din("w2a2", [L, 128, 512])
    g2_d = din("g2", [L, 128, 512])
    consts = din("consts", [128, 9, 128])
    rmask_d = din("rmask", [128, 320])

    yT = dout("yT", [D, NT])
    conv_out = dout("conv_out", [L, 512, NG, 3])
    h_out = dout("h_out", [L, 512, NG])
    shift_out = dout("shift_out", [L, 1792, NG])
    wkv_out = dout("wkv_out", [L, NG, 8, 64, 64])

    xscr = nc.dram_tensor("xscr", [D, NT], F32).ap()

    with ExitStack() as st:
        S = Sched(nc, st, same_engine_sync=(("pool",) if os.environ.get("MK_NOSAME") else ("pool", "act", "dve")))
        op = S.op
        R = {}

        Bx = [S.buf(f"x{i}") for i in range(8)]
        hT = S.sb([128, 8, NT], BF16); Bh = S.buf("h")
        big = S.sb([128, 8 * NT], BF16); Bbig = S.buf("big")
        cst = S.sb([128, 9, 128]); Bc = S.buf("c")
        rmask = S.sb([128, 320]); Brm = S.buf("rm")
        ones_bf = S.sb([128, 128], BF16)
        eps_t = S.sb([128, 4])
        mod = [S.sb([128, 72, NG]) for _ in range(L)]; Bmod = [[S.buf() for _ in range(3)] for _ in range(L)]
        b_ada_t = S.sb([128, L, 72])
        norms_t = S.sb([128, L, 3, 8])
        fnorm_t = S.sb([128, 8])
        fsh_t = S.sb([128, 8, NG])
        fA_t = S.sb([128, 8, NG])
        Bsm = S.buf("smallconst")

        ident = cst[:, 0, :]
        blk1 = cst[:, 1, :]
        m4 = cst[:, 2:6, :]
        mlow = cst[:, 6, :]

        banks = [S.ps([128, 512]) for _ in range(8)]
        Bbank = [S.buf(f"bank{i}", excl=True) for i in range(8)]
        ring = {"i": 0}

        def pbank():
            i = ring["i"]
            ring["i"] = (i + 1) % 8
            return banks[i], Bbank[i]

        NSLOT = 4
        SLOT = 8 * 128
        stg = [S.sb([128, SLOT]) for _ in range(NSLOT)]; Bstg = [S.buf() for _ in range(NSLOT)]
        wbf = [S.sb([128, SLOT], BF16) for _ in range(NSLOT)]; Bwbf = [S.buf() for _ in range(NSLOT)]
        wr = {"i": 0}

        def load_w(dram_ap, nk, ncol=128):
            i = wr["i"]
            wr["i"] = (i + 1) % NSLOT
            n = nk * ncol
            assert n <= SLOT
            sv = stg[i][:, 0:n].rearrange("p (k c) -> p k c", c=ncol)
            S.dma("sp", sv, dram_ap, f"stg{i}", writes=[Bstg[i]])
            op("pool", lambda e: e.tensor_copy(out=wbf[i][:, 0:n], in_=stg[i][:, 0:n]), reads=[Bstg[i]], writes=[Bwbf[i]])
            return wbf[i][:, 0:n].rearrange("p (k c) -> p k c", c=ncol), Bwbf[i]

        def wslice(w2d, col0, ncol=128):
            return w2d.rearrange("(kc p) n -> p kc n", p=128)[:, :, col0:col0 + ncol]

        class Scope:
            def __enter__(self):
                self.es = ExitStack()
                self.prev = S.stack
                S.stack = self.es
                return self

            def __exit__(self, *a):
                S.barrier()
                S.stack = self.prev
                self.es.close()
                return False

        S.dma("sp", cst[:], consts, "c", writes=[Bc])
        S.dma("sp", rmask[:], rmask_d, "rm", writes=[Brm])
        S.dma("sp", b_ada_t[:], b_adaT.rearrange("l p j -> p l j"), "sm", writes=[Bsm])
        S.dma("sp", norms_t[:], normsT.rearrange("l s p k -> p l s k"), "sm", writes=[Bsm])
        S.dma("sp", fnorm_t[:], fnormT, "sm", writes=[Bsm])
        op("pool", lambda e: e.memset(ones_bf[:], 1.0), writes=[Bsm])
        op("pool", lambda e: e.memset(eps_t[:, 0:1], NORM_EPS), writes=[Bsm])
        op("pool", lambda e: e.memset(eps_t[:, 1:2], 1.0), writes=[Bsm])
        op("pool", lambda e: e.memset(eps_t[:, 2:3], GN_EPS), writes=[Bsm])
        op("pool", lambda e: e.memset(eps_t[:, 3:4], 0.0), writes=[Bsm])
        op("pool", lambda e: e.memset(fsh_t[:], 0.0), writes=[Bsm])
        op("dve", lambda e: e.tensor_copy(out=fA_t[:], in_=fnorm_t[:, :].unsqueeze(2).broadcast_to([128, 8, NG])), reads=[Bsm], writes=[Bsm])

        cs32 = S.sb([128, 8, NG]); csb = S.sb([128, 8, NG], BF16); Bcs = S.buf()
        S.dma("sp", cs32[:], cT.rearrange("(k p) g -> p k g", p=128), "sm", writes=[Bcs])
        op("act", lambda e: e.activation(out=csb[:], in_=cs32[:], func=AF.Silu), reads=[Bcs], writes=[Bcs])

        def ada_chunk(l, j, wt, Bw):
            pb, Bp = pbank()
            for kc in range(8):
                op("pe", lambda e: e.matmul(pb[:, 0:NG], lhsT=wt[:, kc, :], rhs=csb[:, kc, :], start=(kc == 0), stop=(kc == 7)), reads=[Bw, Bcs], writes=[Bp])
            op("act", lambda e: e.activation(out=mod[l][:, j, :], in_=pb[:, 0:NG], func=AF.Identity, bias=b_ada_t[:, l, j:j + 1]),
               reads=[Bp, Bsm], writes=[Bmod[l][j // 24]])

        def ada_post(l, s_):
            Bm = Bmod[l][s_]
            scv = mod[l][:, (3 * s_ + 1) * 8:(3 * s_ + 2) * 8, :]
            op("dve", lambda e: e.tensor_scalar(out=scv, in0=scv, scalar1=1.0, scalar2=None, op0=ALU.add), reads=[Bm], writes=[Bm])
            op("dve", lambda e: e.tensor_tensor(out=scv, in0=scv, in1=norms_t[:, l, s_, :].unsqueeze(2).broadcast_to([128, 8, NG]), op=ALU.mult),
               reads=[Bm, Bsm], writes=[Bm])
            if s_ != 1:
                gv = mod[l][:, (3 * s_ + 2) * 8:(3 * s_ + 3) * 8, :]
                op("dve", lambda e: e.tensor_scalar(out=gv, in0=gv, scalar1=0.5, scalar2=None, op0=ALU.mult), reads=[Bm], writes=[Bm])

        pend = [load_w(wslice(w_ada[0], j * 128), 8) for j in range(2)]
        for j in range(24):
            wt, Bw = pend.pop(0)
            if j + 2 < 24:
                pend.append(load_w(wslice(w_ada[0], (j + 2) * 128), 8))
            ada_chunk(0, j, wt, Bw)
        ada_post(0, 0)

        def ada_stream(astg, Bas, awb, Baw, seq, posts):

            def ld(k):
                l, j = seq[k]
                i = k % 2
                S.dma("sp", astg[i][:, :].rearrange("p (k c) -> p k c", c=128), wslice(w_ada[l], j * 128), f"astg{i}", writes=[Bas[i]])
                op("pool", lambda e: e.tensor_copy(out=awb[i][:], in_=astg[i][:]), reads=[Bas[i]], writes=[Baw[i]])
            ld(0)
            for k, (l, j) in enumerate(seq):
                if k + 1 < len(seq):
                    ld(k + 1)
                ada_chunk(l, j, awb[k % 2][:, :].rearrange("p (k c) -> p k c", c=128), Baw[k % 2])
                yield
            for (l, s_) in posts:
                ada_post(l, s_)

        def ffn_with_stream(l, which, G, first, seq, posts):
            with Scope():
                astg = [S.sb([128, SLOT]) for _ in range(2)]; Bas = [S.buf() for _ in range(2)]
                awb = [S.sb([128, SLOT], BF16) for _ in range(2)]; Baw = [S.buf() for _ in range(2)]
                ag = ada_stream(astg, Bas, awb, Baw, seq, posts)
                R["ffn_fill"] = lambda: next(ag, None)
                ffn(l, which, G, first=first)
                R["ffn_fill"] = lambda: None
                for _ in ag:
                    pass

        def bc4(ap3):
            return ap3.unsqueeze(3).broadcast_to([128, ap3.shape[1], NB, 4])

        NT256 = [(i * 256, 256) for i in range(8)] + [(2048, 64)]

        def norm_mod(A, SH, Bpar, out_fn, after=None, nbuf=2):
            x = R["x"]
            with Scope():
                t8s = [(S.sb([128, 8, 256]), S.buf()) for _ in range(nbuf)]
                sqbs = [(S.sb([128, 8, 256], BF16), S.buf()) for _ in range(nbuf)]
                rstds = [(S.sb([128, 256]), S.buf()) for _ in range(nbuf)]
                pbs = {}

                def stA(ti):
                    c0, n = NT256[ti]
                    sqb, Bsq = sqbs[ti % nbuf]
                    op("dve", lambda e: e.tensor_tensor(out=sqb[:, :, 0:n], in0=x[:, :, c0:c0 + n], in1=x[:, :, c0:c0 + n], op=ALU.mult), reads=Bx, writes=[Bsq])
                    pb, Bp = pbank()
                    pbs[ti] = (pb, Bp)
                    for kc in range(8):
                        op("pe", lambda e: e.matmul(pb[:, 0:n], lhsT=ones_bf[:], rhs=sqb[:, kc, 0:n], start=(kc == 0), stop=(kc == 7)), reads=[Bsq, Bsm], writes=[Bp])

                def stB(ti):
                    c0, n = NT256[ti]
                    t8, BtmpA = t8s[ti % nbuf]; rstd, Brs = rstds[ti % nbuf]
                    pb, Bp = pbs.pop(ti)
                    op("act", lambda e: e.activation(out=rstd[:, 0:n], in_=pb[:, 0:n], func=AF.Ln, bias=eps_t[:, 0:1], scale=1.0 / D), reads=[Bp, Bsm], writes=[Brs])
                    op("act", lambda e: e.activation(out=rstd[:, 0:n], in_=rstd[:, 0:n], func=AF.Exp, scale=-0.5), reads=[Brs], writes=[Brs])
                    op("dve", lambda e: e.tensor_tensor(out=t8[:, :, 0:n], in0=x[:, :, c0:c0 + n], in1=rstd[:, 0:n].unsqueeze(1).broadcast_to([128, 8, n]), op=ALU.mult),
                       reads=Bx + [Brs], writes=[BtmpA])

                def stC(ti):
                    c0, n = NT256[ti]
                    t8, BtmpA = t8s[ti % nbuf]
                    o, Bo = out_fn(ti, c0, n)
                    if c0 < NP_:
                        for kc in range(8):
                            op("act", lambda e: e.activation(out=o[:, kc, :], in_=t8[:, kc, 0:n], func=AF.Identity, scale=A[:, kc, 0:1], bias=SH[:, kc, 0:1]),
                               reads=[BtmpA, Bpar], writes=[Bo])
                    else:
                        t4 = t8[:, :, 0:n].rearrange("p k (b t) -> p k b t", t=4)
                        op("dve", lambda e: e.tensor_tensor(out=t4, in0=t4, in1=bc4(A[:, :, 1:NG]), op=ALU.mult), reads=[BtmpA, Bpar], writes=[BtmpA])
                        op("dve", lambda e: e.tensor_tensor(out=o.rearrange("p k (b t) -> p k b t", t=4), in0=t4, in1=bc4(SH[:, :, 1:NG]), op=ALU.add),
                           reads=[BtmpA, Bpar], writes=[Bo])
                    if after is not None:
                        after(ti, c0, n)

                nt = len(NT256)
                if nbuf < 2:
                    for ti in range(nt):
                        stA(ti); stB(ti); stC(ti)
                else:
                    stA(0)
                    for ti in range(nt):
                        if ti + 1 < nt:
                            stA(ti + 1)
                        stB(ti)
                        stC(ti)

        def h_out_fn(ti, c0, n):
            return hT[:, :, c0:c0 + n], Bh

        def resid_add(pb, Bp, i, c0, n, G, Bpar):
            x = R["x"]; ptmp = R["ptmp"]; Bpt = R["Bpt"]
            if c0 < NP_:
                op("dve", lambda e: e.scalar_tensor_tensor(out=x[:, i, c0:c0 + n], in0=pb[:, 0:n], scalar=G[:, i, 0:1], in1=x[:, i, c0:c0 + n], op0=ALU.mult, op1=ALU.add),
                   reads=[Bp, Bpar, Bx[i]], writes=[Bx[i]])
            else:
                op("dve", lambda e: e.tensor_tensor(out=ptmp[:, 0:n].rearrange("p (b t) -> p b t", t=4), in0=pb[:, 0:n].rearrange("p (b t) -> p b t", t=4),
                                                    in1=G[:, i, 1:NG].unsqueeze(2).broadcast_to([128, NB, 4]), op=ALU.mult), reads=[Bp, Bpar], writes=[Bpt])
                op("dve", lambda e: e.tensor_tensor(out=x[:, i, c0:c0 + n], in0=x[:, i, c0:c0 + n], in1=ptmp[:, 0:n], op=ALU.add), reads=[Bpt, Bx[i]], writes=[Bx[i]])

        FPARTS = [(0, 8), (8, 7), (15, 7)]

        def ffn_prefetch(l, which):
            return [(load_w(wslice(wg_d[which][l], 0), 8), load_w(wslice(wu_d[which][l], 0), 8))]

        def ffn(l, which, G, first=None):
            act3 = big[:, :].rearrange("p (j t) -> p j t", t=NT)
            with Scope():
                sgt = [S.sb([128, 512]) for _ in range(2)]; Bsg = [S.buf() for _ in range(2)]
                for (j0, nj) in FPARTS:
                    jlist = list(range(j0, j0 + nj))
                    if j0 == 0 and first is not None:
                        pend = first
                    else:
                        pend = [(load_w(wslice(wg_d[which][l], jlist[0] * 128), 8), load_w(wslice(wu_d[which][l], jlist[0] * 128), 8))]
                    for jj, j in enumerate(jlist):
                        (wg, Bwg), (wu, Bwu) = pend.pop(0)
                        if jj + 1 < nj:
                            j2 = jlist[jj + 1]
                            pend.append((load_w(wslice(wg_d[which][l], j2 * 128), 8), load_w(wslice(wu_d[which][l], j2 * 128), 8)))
                        for ti, (c0, n) in enumerate(TT):
                            pg, Bpg = pbank()
                            pu, Bpu = pbank()
                            for kc in range(8):
                                op("pe", lambda e, kc=kc, pg=pg, wg=wg, c0=c0, n=n: e.matmul(pg[:, 0:n], lhsT=wg[:, kc, :], rhs=hT[:, kc, c0:c0 + n], start=(kc == 0), stop=(kc == 7)),
                                   reads=[Bwg, Bh], writes=[Bpg])
                            for kc in range(8):
                                op("pe", lambda e, kc=kc, pu=pu, wu=wu, c0=c0, n=n: e.matmul(pu[:, 0:n], lhsT=wu[:, kc, :], rhs=hT[:, kc, c0:c0 + n], start=(kc == 0), stop=(kc == 7)),
                                   reads=[Bwu, Bh], writes=[Bpu])
                            sg, Bs_ = sgt[ti % 2], Bsg[ti % 2]
                            op("act", lambda e, pg=pg, sg=sg, n=n: e.activation(out=sg[:, 0:n], in_=pg[:, 0:n], func=AF.Silu), reads=[Bpg], writes=[Bs_])
                            op("dve", lambda e, pu=pu, sg=sg, jj=jj, c0=c0, n=n: e.tensor_tensor(out=act3[:, jj, c0:c0 + n], in0=pu[:, 0:n], in1=sg[:, 0:n], op=ALU.mult),
                               reads=[Bpu, Bs_], writes=[Bbig])
                            R.get("ffn_fill", lambda: None)()
                    wdv = wd_d[which][l].rearrange("(fc p) n -> p fc n", p=128)
                    pend = [load_w(wdv[:, j0:j0 + nj, 0:128], nj)]
                    for i in range(8):
                        wd, Bwd = pend.pop(0)
                        if i + 1 < 8:
                            pend.append(load_w(wdv[:, j0:j0 + nj, (i + 1) * 128:(i + 2) * 128], nj))
                        for ti, (c0, n) in enumerate(TT):
                            pb, Bp = pbank()
                            for jj in range(nj):
                                op("pe", lambda e, jj=jj, pb=pb, wd=wd, c0=c0, n=n: e.matmul(pb[:, 0:n], lhsT=wd[:, jj, :], rhs=act3[:, jj, c0:c0 + n], start=(jj == 0), stop=(jj == nj - 1)),
                                   reads=[Bwd, Bbig], writes=[Bp])
                            resid_add(pb, Bp, i, c0, n, G, Bmod[l][2 * which])
                            R.get("ffn_fill", lambda: None)()

        ymix = big[:, 0:8 * NT].rearrange("p (j t) -> p j t", t=NT)
        MT = [(i * 256, 256) for i in range(8)] + [(2048, 64)]
        LASTP = 7
        SMP = 8
        SW = 264

        def mixer(l):
            lv = S.sb([128, 8, 4]); rv = S.sb([128, 7, 4]); muv = S.sb([128, 14]); Bpv = S.buf()
            S.dma("sp", lv[:], lru_vec[l], "sm", writes=[Bpv])
            S.dma("sp", rv[:], rw_vec[l], "sm", writes=[Bpv])
            S.dma("sp", muv[:], mu_T[l], "sm", writes=[Bpv])
            lsc = S.sb([128, 3, 4])
            op("act", lambda e: e.activation(out=lsc[:, 0, :], in_=lv[:, 7, :], func=AF.Exp, scale=-1.0), reads=[Bpv], writes=[Bpv])
            op("act", lambda e: e.activation(out=lsc[:, 0, :], in_=lsc[:, 0, :], func=AF.Ln, bias=eps_t[:, 1:2], scale=1.0), reads=[Bpv, Bsm], writes=[Bpv])
            op("dve", lambda e: e.tensor_scalar(out=lsc[:, 1, :], in0=lsc[:, 0, :], scalar1=-8.0, scalar2=None, op0=ALU.mult), reads=[Bpv], writes=[Bpv])
            op("dve", lambda e: e.tensor_scalar(out=lsc[:, 2, :], in0=lsc[:, 0, :], scalar1=-16.0, scalar2=None, op0=ALU.mult), reads=[Bpv], writes=[Bpv])
            bdw = S.sb([128, 8 * 128], BF16); Bbd = S.buf()
            for q in range(2):
                wt, Bw = load_w(bd_gate[l, q].rearrange("c p n -> p c n"), 4)
                op("pool", lambda e, q=q, wt=wt: e.tensor_copy(out=bdw[:, q * 512:(q + 1) * 512].rearrange("p (c n) -> p c n", n=128), in_=wt), reads=[Bw], writes=[Bbd])
            bd4 = bdw[:, :].rearrange("p (q c n) -> p q c n", q=2, c=4)
            w2a2b = S.sb([128, 512], BF16); g2b = S.sb([128, 512], BF16); Blr = S.buf()
            wt, Bw = load_w(w2a2[l].rearrange("p (k n) -> p k n", k=1), 1, 512)
            op("pool", lambda e, wt=wt: e.tensor_copy(out=w2a2b[:], in_=wt[:, 0, :]), reads=[Bw], writes=[Blr])
            wt, Bw = load_w(g2_d[l].rearrange("p (k n) -> p k n", k=1), 1, 512)
            op("pool", lambda e, wt=wt: e.tensor_copy(out=g2b[:], in_=wt[:, 0, :]), reads=[Bw], writes=[Blr])

            conv_o = S.sb([128, 4, NG, 3]); hO = S.sb([128, 4, NG]); shO = S.sb([128, 14, NG]); Bout = S.buf()
            conv_i = S.sb([128, 4, NB, 3]); hI = S.sb([128, 4, NB]); shI = S.sb([128, 14, NB]); Bin = S.buf()
            S.dma("sp", conv_i[:], conv_st[l].rearrange("(c p) b j -> p c b j", p=128), "sm", writes=[Bin])
            S.dma("sp", hI[:], h_st[l].rearrange("(c p) b -> p c b", p=128), "sm", writes=[Bin])
            S.dma("sp", shI[:], shift_st[l].rearrange("(c p) b -> p c b", p=128), "sm", writes=[Bin])

            def proj(ws, Bws, c0, n):
                pb, Bp = pbank()
                for kc in range(8):
                    op("pe", lambda e, kc=kc, pb=pb: e.matmul(pb[:, 0:n], lhsT=ws[:, kc, :], rhs=hT[:, kc, c0:c0 + n], start=(kc == 0), stop=(kc == 7)),
                       reads=[Bws, Bh], writes=[Bp])
                return pb, Bp

            LR1 = S.sb([128, NT], BF16); LR2 = S.sb([128, NT], BF16); BLRs = [S.buf(), S.buf()]
            BLR = BLRs[0]
            with Scope():
                Ts = [{k: (S.sb([128, SW]), S.buf()) for k in ["xpad", "xc", "gbr", "gx", "ga", "a", "bb", "hs", "u"]} for _ in range(4)]
                xcbs = [(S.sb([128, 256], BF16), S.buf()) for _ in range(4)]
                cars = [(S.sb([128, 3]), S.sb([128, 1]), S.buf()) for _ in range(4)]
                wxs = []; wgs = []
                for c in range(4):
                    for (lst, col0) in ((wxs, c * 128), (wgs, 512 + c * 128)):
                        wt, Bw = load_w(wslice(w_in[l], col0), 8)
                        dt_ = S.sb([128, 8, 128], BF16); Bd_ = S.buf()
                        op("pool", lambda e: e.tensor_copy(out=dt_[:], in_=wt), reads=[Bw], writes=[Bd_])
                        lst.append((dt_, Bd_))
                def m1_stream(c):
                    T = Ts[c]; xcb, Bxcb = xcbs[c]; carry3, hprev, Bcar = cars[c]
                    wx, Bwx = wxs[c]; wgt, Bwg = wgs[c]
                    op("pool", lambda e: e.memset(carry3[:], 0.0), writes=[Bcar])
                    op("pool", lambda e: e.memset(hprev[:], 0.0), writes=[Bcar])
                    cw = [lv[:, j, c:c + 1] for j in range(4)]
                    for ti, (c0, n) in enumerate(MT):
                        smp = ti == SMP
                        xp, Bxp = T["xpad"]; xc, Bxc = T["xc"]; gbr, Bgb = T["gbr"]; gx, Bgx = T["gx"]; ga, Bga = T["ga"]
                        a_, Ba = T["a"]; bb, Bbb = T["bb"]; hs, Bhs = T["hs"]; u_, Bu = T["u"]
                        pbx, Bpx = proj(wx, Bwx, c0, n)
                        if not smp:
                            op("act", lambda e: e.activation(out=xp[:, 3:3 + n], in_=pbx[:, 0:n], func=AF.Copy), reads=[Bpx], writes=[Bxp])
                            yield
                            op("dve", lambda e: e.tensor_copy(out=xp[:, 0:3], in_=carry3[:]), reads=[Bcar], writes=[Bxp])
                            yield
                            op("dve", lambda e: e.tensor_copy(out=carry3[:], in_=xp[:, n:n + 3]), reads=[Bxp], writes=[Bcar])
                            yield
                            xv = [xp[:, j:j + n] for j in range(4)]
                            xcv = xc[:, 0:n]
                        else:
                            xp3 = xp[:, 0:NB * 7].rearrange("p (b j) -> p b j", j=7)
                            op("act", lambda e: e.activation(out=xp3[:, :, 3:7], in_=pbx[:, 0:n].rearrange("p (b t) -> p b t", t=4), func=AF.Copy), reads=[Bpx], writes=[Bxp])
                            yield
                            op("dve", lambda e: e.tensor_copy(out=xp3[:, :, 0:3], in_=conv_i[:, c, :, :]), reads=[Bin], writes=[Bxp])
                            yield
                            xv = [xp3[:, :, j:j + 4] for j in range(4)]
                            xcv = xc[:, 0:n].rearrange("p (b t) -> p b t", t=4)
                        pbg_, Bpg_ = proj(wgt, Bwg, c0, n)
                        op("act", lambda e: e.activation(out=gbr[:, 0:n], in_=pbg_[:, 0:n], func=AF.Copy), reads=[Bpg_], writes=[Bgb])
                        yield
                        op("dve", lambda e: e.tensor_scalar(out=xcv, in0=xv[3], scalar1=cw[3], scalar2=lv[:, 4, c:c + 1], op0=ALU.mult, op1=ALU.add), reads=[Bxp, Bpv], writes=[Bxc])
                        yield
                        for j in range(3):
                            op("dve", lambda e, j=j: e.scalar_tensor_tensor(out=xcv, in0=xv[j], scalar=cw[j], in1=xcv, op0=ALU.mult, op1=ALU.add), reads=[Bxp, Bpv, Bxc], writes=[Bxc])
                            yield
                        op("act", lambda e: e.activation(out=xcb[:, 0:n], in_=xc[:, 0:n], func=AF.Copy), reads=[Bxc], writes=[Bxcb])
                        yield
                        if ti == LASTP:
                            op("pool", lambda e: e.tensor_copy(out=conv_o[:, c, 0, :], in_=xp[:, n:n + 3]), reads=[Bxp], writes=[Bout])
                            yield
                        if smp:
                            op("pool", lambda e: e.tensor_copy(out=conv_o[:, c, 1:NG, :], in_=xp3[:, :, 4:7]), reads=[Bxp], writes=[Bout])
                            yield
                        p1, Bp1 = pbank()
                        op("pe", lambda e: e.matmul(p1[:, 0:n], lhsT=bd4[:, 0, c, :], rhs=xcb[:, 0:n], start=True, stop=True), reads=[Bbd, Bxcb], writes=[Bp1])
                        p2, Bp2 = pbank()
                        op("pe", lambda e: e.matmul(p2[:, 0:n], lhsT=bd4[:, 1, c, :], rhs=xcb[:, 0:n], start=True, stop=True), reads=[Bbd, Bxcb], writes=[Bp2])
                        op("act", lambda e: e.activation(out=gx[:, 0:n], in_=p1[:, 0:n], func=AF.Sigmoid, bias=lv[:, 5, c:c + 1]), reads=[Bp1, Bpv], writes=[Bgx])
                        op("act", lambda e: e.activation(out=ga[:, 0:n], in_=p2[:, 0:n], func=AF.Sigmoid, bias=lv[:, 6, c:c + 1]), reads=[Bp2, Bpv], writes=[Bga])
                        yield
                        op("act", lambda e: e.activation(out=a_[:, 0:n], in_=ga[:, 0:n], func=AF.Exp, scale=lsc[:, 1, c:c + 1]), reads=[Bga, Bpv], writes=[Ba])
                        yield
                        op("act", lambda e: e.activation(out=bb[:, 0:n], in_=ga[:, 0:n], func=AF.Exp, scale=lsc[:, 2, c:c + 1]), reads=[Bga, Bpv], writes=[Bbb])
                        yield
                        op("act", lambda e: e.activation(out=bb[:, 0:n], in_=bb[:, 0:n], func=AF.Sqrt, bias=eps_t[:, 1:2], scale=-1.0), reads=[Bbb, Bsm], writes=[Bbb])
                        yield
                        op("dve", lambda e: e.tensor_tensor(out=bb[:, 0:n], in0=bb[:, 0:n], in1=gx[:, 0:n], op=ALU.mult), reads=[Bbb, Bgx], writes=[Bbb])
                        yield
                        op("dve", lambda e: e.tensor_tensor(out=bb[:, 0:n], in0=bb[:, 0:n], in1=xc[:, 0:n], op=ALU.mult), reads=[Bbb, Bxc], writes=[Bbb])
                        yield
                        if not smp:
                            op("dve", lambda e: e.tensor_tensor_scan(out=hs[:, 0:n], data0=a_[:, 0:n], data1=bb[:, 0:n], initial=hprev[:, 0:1], op0=ALU.mult, op1=ALU.add),
                               reads=[Ba, Bbb, Bcar], writes=[Bhs])
                            yield
                            op("dve", lambda e: e.tensor_copy(out=hprev[:], in_=hs[:, n - 1:n]), reads=[Bhs], writes=[Bcar])
                            yield
                            if ti == LASTP:
                                op("pool", lambda e: e.tensor_copy(out=hO[:, c, 0:1], in_=hs[:, n - 1:n]), reads=[Bhs], writes=[Bout])
                                yield
                        else:
                            a3 = a_[:, 0:n].rearrange("p (b t) -> p b t", t=4)
                            b3 = bb[:, 0:n].rearrange("p (b t) -> p b t", t=4)
                            op("dve", lambda e: e.tensor_tensor(out=u_[:, 0:NB], in0=a3[:, :, 0], in1=hI[:, c, :], op=ALU.mult), reads=[Ba, Bin], writes=[Bu])
                            yield
                            op("dve", lambda e: e.tensor_tensor(out=b3[:, :, 0], in0=b3[:, :, 0], in1=u_[:, 0:NB], op=ALU.add), reads=[Bu, Bbb], writes=[Bbb])
                            yield
                            op("dve", lambda e: e.memset(a3[:, :, 0], 0.0), reads=[Bu], writes=[Ba])
                            yield
                            op("dve", lambda e: e.tensor_tensor_scan(out=hs[:, 0:n], data0=a_[:, 0:n], data1=bb[:, 0:n], initial=0.0, op0=ALU.mult, op1=ALU.add),
                               reads=[Ba, Bbb], writes=[Bhs])
                            yield
                            op("pool", lambda e: e.tensor_copy(out=hO[:, c, 1:NG], in_=hs[:, 0:n].rearrange("p (b t) -> p b t", t=4)[:, :, 3]), reads=[Bhs], writes=[Bout])
                            yield
                        op("pool", lambda e: e.tensor_tensor(out=u_[:, 0:n], in0=gbr[:, 0:n], in1=gbr[:, 0:n], op=ALU.mult), reads=[Bgb], writes=[Bu])
                        yield
                        op("dve", lambda e: e.tensor_scalar(out=u_[:, 0:n], in0=u_[:, 0:n], scalar1=0.044715, scalar2=1.0, op0=ALU.mult, op1=ALU.add), reads=[Bu], writes=[Bu])
                        yield
                        op("dve", lambda e: e.tensor_tensor(out=u_[:, 0:n], in0=u_[:, 0:n], in1=gbr[:, 0:n], op=ALU.mult), reads=[Bu, Bgb], writes=[Bu])
                        yield
                        op("act", lambda e: e.activation(out=u_[:, 0:n], in_=u_[:, 0:n], func=AF.Sigmoid, scale=1.5957691216057308), reads=[Bu], writes=[Bu])
                        yield
                        op("dve", lambda e: e.tensor_tensor(out=u_[:, 0:n], in0=u_[:, 0:n], in1=gbr[:, 0:n], op=ALU.mult), reads=[Bu, Bgb], writes=[Bu])
                        yield
                        op("dve", lambda e: e.tensor_tensor(out=ymix[:, c, c0:c0 + n], in0=u_[:, 0:n], in1=hs[:, 0:n], op=ALU.mult), reads=[Bu, Bhs], writes=[Bbig])
                        yield

                def m2_stream(jch):
                    wt, Bw = load_w(wslice(w_in[l], 1024 + jch * 128), 8)
                    ws = S.sb([128, 8, 128], BF16); Bws = S.buf()
                    op("pool", lambda e: e.tensor_copy(out=ws[:], in_=wt), reads=[Bw], writes=[Bws])
                    prp = S.sb([128, SW]); Bprp = S.buf()
                    dtm = S.sb([128, 256]); Bdt = S.buf()
                    xs0 = S.sb([128, 256]); Bxs0 = S.buf()
                    car = S.sb([128, 1]); Bca = S.buf()
                    Bl = BLRs[jch - 12]
                    op("pool", lambda e: e.memset(car[:], 0.0), writes=[Bca])
                    yield
                    for ti, (c0, n) in enumerate(MT):
                        pb, Bp = proj(ws, Bws, c0, n)
                        if ti != SMP:
                            op("act", lambda e: e.activation(out=prp[:, 1:1 + n], in_=pb[:, 0:n], func=AF.Copy), reads=[Bp], writes=[Bprp])
                            yield
                            op("dve", lambda e: e.tensor_copy(out=prp[:, 0:1], in_=car[:]), reads=[Bca], writes=[Bprp])
                            op("dve", lambda e: e.tensor_copy(out=car[:], in_=prp[:, n:n + 1]), reads=[Bprp], writes=[Bca])
                            if ti == LASTP:
                                op("pool", lambda e: e.tensor_copy(out=shO[:, jch, 0:1], in_=prp[:, n:n + 1]), reads=[Bprp], writes=[Bout])
                            yield
                            cur, prev, dv, ov = prp[:, 1:1 + n], prp[:, 0:n], dtm[:, 0:n], xs0[:, 0:n]
                        else:
                            p3 = prp[:, 0:NB * 5].rearrange("p (b j) -> p b j", j=5)
                            op("act", lambda e: e.activation(out=p3[:, :, 1:5], in_=pb[:, 0:n].rearrange("p (b t) -> p b t", t=4), func=AF.Copy), reads=[Bp], writes=[Bprp])
                            yield
                            op("dve", lambda e: e.tensor_copy(out=p3[:, :, 0], in_=shI[:, jch, :]), reads=[Bin], writes=[Bprp])
                            op("pool", lambda e: e.tensor_copy(out=shO[:, jch, 1:NG], in_=p3[:, :, 4]), reads=[Bprp], writes=[Bout])
                            yield
                            cur, prev = p3[:, :, 1:5], p3[:, :, 0:4]
                            dv = dtm[:, 0:n].rearrange("p (b t) -> p b t", t=4)
                            ov = xs0[:, 0:n].rearrange("p (b t) -> p b t", t=4)
                        op("dve", lambda e: e.tensor_tensor(out=dv, in0=prev, in1=cur, op=ALU.subtract), reads=[Bprp], writes=[Bdt])
                        yield
                        op("dve", lambda e: e.scalar_tensor_tensor(out=ov, in0=dv, scalar=muv[:, jch:jch + 1], in1=cur, op0=ALU.mult, op1=ALU.add), reads=[Bdt, Bprp, Bpv], writes=[Bxs0])
                        yield
                        if jch == 12:
                            op("act", lambda e: e.activation(out=LR1[0:64, c0:c0 + n], in_=xs0[0:64, 0:n], func=AF.Tanh), reads=[Bxs0], writes=[Bl])
                            op("act", lambda e: e.activation(out=LR1[64:128, c0:c0 + n], in_=xs0[64:128, 0:n], func=AF.Copy), reads=[Bxs0], writes=[Bl])
                        else:
                            op("act", lambda e: e.activation(out=LR2[:, c0:c0 + n], in_=xs0[:, 0:n], func=AF.Sigmoid), reads=[Bxs0], writes=[Bl])
                        yield

                streams = [m1_stream(c) for c in range(4)] + [m2_stream(12), m2_stream(13)]
                alive = list(streams)
                while alive:
                    for g in list(alive):
                        try:
                            next(g)
                        except StopIteration:
                            alive.remove(g)
                S.dma("sp", conv_out[l].rearrange("(c p) g j -> p c g j", p=128), conv_o[:], "o1", reads=[Bout])
                S.dma("sp", h_out[l].rearrange("(c p) g -> p c g", p=128), hO[:], "o1", reads=[Bout])
            R["ck"](5)

            with Scope():
                prp = S.sb([128, SW]); Bprp = S.buf()
                dtm = S.sb([128, 256]); Bdt = S.buf()
                carry1 = S.sb([128, 14]); Bc1 = S.buf()
                op("pool", lambda e: e.memset(carry1[:], 0.0), writes=[Bc1])

                def proj_shift(jch, ws, Bws, ti, c0, n, out, Bo):
                    pb, Bp = proj(ws, Bws, c0, n)
                    if ti != SMP:
                        op("act", lambda e: e.activation(out=prp[:, 1:1 + n], in_=pb[:, 0:n], func=AF.Copy), reads=[Bp], writes=[Bprp])
                        op("dve", lambda e: e.tensor_copy(out=prp[:, 0:1], in_=carry1[:, jch:jch + 1]), reads=[Bc1], writes=[Bprp])
                        op("dve", lambda e: e.tensor_copy(out=carry1[:, jch:jch + 1], in_=prp[:, n:n + 1]), reads=[Bprp], writes=[Bc1])
                        if ti == LASTP:
                            op("pool", lambda e: e.tensor_copy(out=shO[:, jch, 0:1], in_=prp[:, n:n + 1]), reads=[Bprp], writes=[Bout])
                        cur, prev, dv, ov = prp[:, 1:1 + n], prp[:, 0:n], dtm[:, 0:n], out[:, 0:n]
                    else:
                        p3 = prp[:, 0:NB * 5].rearrange("p (b j) -> p b j", j=5)
                        op("act", lambda e: e.activation(out=p3[:, :, 1:5], in_=pb[:, 0:n].rearrange("p (b t) -> p b t", t=4), func=AF.Copy), reads=[Bp], writes=[Bprp])
                        op("dve", lambda e: e.tensor_copy(out=p3[:, :, 0], in_=shI[:, jch, :]), reads=[Bin], writes=[Bprp])
                        op("pool", lambda e: e.tensor_copy(out=shO[:, jch, 1:NG], in_=p3[:, :, 4]), reads=[Bprp], writes=[Bout])
                        cur, prev = p3[:, :, 1:5], p3[:, :, 0:4]
                        dv = dtm[:, 0:n].rearrange("p (b t) -> p b t", t=4)
                        ov = out[:, 0:n].rearrange("p (b t) -> p b t", t=4)
                    op("dve", lambda e: e.tensor_tensor(out=dv, in0=prev, in1=cur, op=ALU.subtract), reads=[Bprp], writes=[Bdt])
                    op("dve", lambda e: e.scalar_tensor_tensor(out=ov, in0=dv, scalar=muv[:, jch:jch + 1], in1=cur, op0=ALU.mult, op1=ALU.add), reads=[Bdt, Bprp, Bpv], writes=[Bo])

                R["ck"](6)
                names = ["r", "k", "v", "lw", "cum", "ecp", "ecn", "ar", "kk", "g", "bon", "t1"]
                Wsets = [{k: (S.sb([128, 256]), S.buf()) for k in names} for _ in range(2)]
                WBsets = [{k: (S.sb([128, 256], BF16), S.buf()) for k in ["RT", "KT", "AT", "BT", "V"]} for _ in range(2)]
                yfm_t = (S.sb([128, 256]), S.buf())
                ST = S.sb([128, 64]); STb = S.sb([128, 64], BF16); BST = S.buf()
                STs = S.sb([128, NB, 64]); STsb = S.sb([128, NB, 64], BF16); BSTs = S.buf()
                identb = S.sb([128, 128], BF16); Bidb = S.buf()
                op("act", lambda e: e.activation(out=identb[:], in_=ident, func=AF.Copy), reads=[Bc], writes=[Bidb])
                TM = S.sb([128, 4 * 384], BF16); BTM = S.buf()
                SCt = [(None if i_ == 1 else S.sb([128, 512], BF16)) for i_ in range(4)]; BSC = [S.buf() for _ in range(4)]
                PPt = [[S.sb([128, 512]) for _ in range(2)] for _ in range(2)]; BPP = [[S.buf() for _ in range(2)] for _ in range(2)]
                Tt = [[S.sb([128, 256]) for _ in range(2)] for _ in range(2)]; BT_ = [[S.buf() for _ in range(2)] for _ in range(2)]
                U0t = S.sb([128, 512]); BU0 = S.buf()
                WFsets = [{k: (S.sb([128, 256]), S.buf()) for k in ["ATf", "BTf"]} for _ in range(2)]
                UTt = S.sb([128, 512], BF16); BUT = S.buf()
                ytm = S.sb([128, 16, 128]); Bytm = S.buf()
                gst = S.sb([128, 32, 4]); Bgst = S.buf()
                S_sq = S.sb([128, 512]); Bsqs = S.buf()
                nrv = S.sb([128, 7, 4])
                op("dve", lambda e: e.tensor_scalar(out=nrv[:], in0=rv[:], scalar1=-1.0, scalar2=None, op0=ALU.mult), reads=[Bpv], writes=[Bpv])

                if os.environ.get("MK_MEM"):
                    print("M3 scope SBUF bytes remaining:", nc.sbuf_bytes_remaining)

                def wkv_group(chunks, C, units_list, ci0, st):
                    nch = len(chunks); nq = 2 * nch
                    WB = WBsets[st]; WF = WFsets[st]
                    fill = R["fill"]
                    RT, KT, AT, BT, V = (WB[k][0] for k in ["RT", "KT", "AT", "BT", "V"])
                    BRT, BKT, BAT, BBT, BV = (WB[k][1] for k in ["RT", "KT", "AT", "BT", "V"])

                    def v3(t):
                        return t[0:C, 0:nq * C].rearrange("p (q c) -> p q c", c=C)

                    def bcm(mk):
                        return mk[0:C, 0:C].unsqueeze(1).broadcast_to([C, nq, C])
                    TMv = TM[0:C, 0:nch * 384].rearrange("p (h i f) -> p h i f", i=3, f=128)
                    for h0 in range(0, nch, 2):
                        hn = min(2, nch - h0)
                        for hh in range(hn):
                            tp, Btp = pbank()
                            tpb = tp[0:C, 0:384].rearrange("p (i f) -> p i f", f=128)
                            c0, c1 = chunks[h0 + hh]
                            for i, (src, Bsrc) in enumerate([(KT, BKT), (BT, BBT), (V, BV)]):
                                op("pe", lambda e: e.matmul(tpb[:, i, :], lhsT=src[:, c0:c1], rhs=identb[:], start=True, stop=True), reads=[Bsrc, Bidb], writes=[Btp])
                            op("act", lambda e: e.activation(out=TMv[:, h0 + hh], in_=tpb, func=AF.Copy), reads=[Btp], writes=[BTM])
                    R["ck"](6.9)
                    ATf, BATf = WF["ATf"]; BTf, BBTf = WF["BTf"]
                    pairs = [(KT, AT, BKT, BAT), (BTf, ATf, BBTf, BATf), (KT, RT, BKT, BRT), (BT, RT, BBT, BRT), (ATf, BTf, BATf, BBTf)]
                    masks = [cst[:, 2, :], cst[:, 2, :], cst[:, 4, :], cst[:, 4, :], cst[:, 6, :]]

                    def ppv(j, pp):
                        return PPt[j][pp][0:C, 0:2 * nch * C].rearrange("p (t h c) -> p t h c", t=2, c=C)

                    def tv(j, pp):
                        return Tt[j][pp][0:C, 0:nch * C].rearrange("p (h c) -> p h c", c=C)
                    for i in range(5):
                        lh, rh, Bl, Br = pairs[i]
                        for j in range(2):
                            pj = slice(64 * j, 64 * j + 64)
                            sb_, Bsb = pbank()
                            sbv = sb_[0:C, 0:nch * C].rearrange("p (h c) -> p h c", c=C)
                            for ch in range(nch):
                                c0, c1 = chunks[ch]
                                op("pe", lambda e: e.matmul(sbv[:, ch, :], lhsT=lh[pj, c0:c1], rhs=rh[pj, c0:c1], start=True, stop=True), reads=[Bl, Br], writes=[Bsb])
                            if i == 1:
                                dv, Bd = ppv(j, 0)[:, 1], BPP[j][0]
                            elif i == 4:
                                dv, Bd = ppv(j, 0)[:, 0], BPP[j][0]
                            else:
                                dv, Bd = SCt[i][0:C, 0:nq * C].rearrange("p (h j c) -> p h j c", j=2, c=C)[:, :, j, :], BSC[i]
                            op("dve", lambda e: e.tensor_tensor(out=dv, in0=sbv, in1=masks[i][0:C, 0:C].unsqueeze(1).broadcast_to([C, nch, C]), op=ALU.mult), reads=[Bsb, Bc], writes=[Bd])
                        fill()
                    R["ck"](7.1)
                    m = 1
                    while (1 << m) < C:
                        m += 1
                    for j in range(2):
                        op("dve", lambda e: e.tensor_tensor(out=tv(j, 0), in0=ppv(j, 0)[:, 1], in1=ident[0:C, 0:C].unsqueeze(1).broadcast_to([C, nch, C]), op=ALU.add),
                           reads=[BPP[j][0], Bc], writes=[BT_[j][0]])
                    tcur = 0
                    for i in range(1, m):
                        cur, nxt = (i - 1) % 2, i % 2
                        last = (i == m - 1)
                        bk = []
                        for j in range(2):
                            pa, Bpa = pbank()
                            pav = pa[0:C, 0:2 * nch * C].rearrange("p (t h c) -> p t h c", t=2, c=C)
                            for ch in range(nch):
                                op("pe", lambda e: e.matmul(pav[:, 0, ch, :], lhsT=ppv(j, cur)[:, 1, ch, :], rhs=ppv(j, cur)[:, 0, ch, :], start=True, stop=True), reads=[BPP[j][cur]], writes=[Bpa])
                            if not last:
                                for ch in range(nch):
                                    op("pe", lambda e: e.matmul(pav[:, 1, ch, :], lhsT=ppv(j, cur)[:, 0, ch, :], rhs=ppv(j, cur)[:, 1, ch, :], start=True, stop=True), reads=[BPP[j][cur]], writes=[Bpa])
                            bk.append((pav, Bpa))
                        for j in range(2):
                            pav, Bpa = bk[j]
                            nt_ = 1 if last else 2
                            if j == 0:
                                op("act", lambda e: e.activation(out=ppv(j, nxt)[:, 0:nt_], in_=pav[:, 0:nt_], func=AF.Copy), reads=[Bpa], writes=[BPP[j][nxt]])
                            else:
                                op("dve", lambda e: e.tensor_copy(out=ppv(j, nxt)[:, 0:nt_], in_=pav[:, 0:nt_]), reads=[Bpa], writes=[BPP[j][nxt]])
                        for j in range(2):
                            pt_, Bpt_ = pbank()
                            ptv = pt_[0:C, 0:nch * C].rearrange("p (h c) -> p h c", c=C)
                            for ch in range(nch):
                                op("pe", lambda e: e.matmul(ptv[:, ch, :], lhsT=ppv(j, nxt)[:, 0, ch, :], rhs=tv(j, tcur)[:, ch, :], start=True, stop=True), reads=[BPP[j][nxt], BT_[j][tcur]], writes=[Bpt_])
                            op("dve", lambda e: e.tensor_tensor(out=tv(j, 1 - tcur), in0=ptv, in1=tv(j, tcur), op=ALU.add), reads=[Bpt_, BT_[j][tcur]], writes=[BT_[j][1 - tcur]])
                        fill()
                        tcur = 1 - tcur
                    R["ck"](7.2)
                    for units in units_list:
                        nu = len(units)
                        pu0, Bpu0 = pbank()
                        for u, (ch, Sf, Sb, BS, wc) in enumerate(units):
                            c0, c1 = chunks[ch]
                            for j in range(2):
                                pj = slice(64 * j, 64 * j + 64)
                                oc = slice((u * 2 + j) * 64, (u * 2 + j) * 64 + 64)
                                q = ch * 2 + j
                                op("pe", lambda e: e.matmul(pu0[0:C, oc], lhsT=AT[pj, c0:c1], rhs=Sb[pj, :], start=True, stop=False), reads=[BAT, BS], writes=[Bpu0])
                                op("pe", lambda e: e.matmul(pu0[0:C, oc], lhsT=v3(SCt[0])[:, q, :], rhs=TMv[:, ch, 2, pj], start=False, stop=True), reads=[BSC[0], BTM], writes=[Bpu0])
                        op("act", lambda e: e.activation(out=U0t[0:C, 0:nu * 128], in_=pu0[0:C, 0:nu * 128], func=AF.Copy), reads=[Bpu0], writes=[BU0])
                        pu, Bpu = pbank()
                        for u, (ch, Sf, Sb, BS, wc) in enumerate(units):
                            for j in range(2):
                                oc = slice((u * 2 + j) * 64, (u * 2 + j) * 64 + 64)
                                q = ch * 2 + j
                                op("pe", lambda e: e.matmul(pu[0:C, oc], lhsT=tv(j, tcur)[:, ch, :], rhs=U0t[0:C, oc], start=True, stop=True), reads=[BT_[j][tcur], BU0], writes=[Bpu])
                        op("act", lambda e: e.activation(out=UTt[0:C, 0:nu * 128], in_=pu[0:C, 0:nu * 128], func=AF.Copy), reads=[Bpu], writes=[BUT])
                        yp, Byp = pbank()
                        for u, (ch, Sf, Sb, BS, wc) in enumerate(units):
                            c0, c1 = chunks[ch]
                            for j in range(2):
                                pj = slice(64 * j, 64 * j + 64)
                                oc = slice((u * 2 + j) * 64, (u * 2 + j) * 64 + 64)
                                q = ch * 2 + j
                                op("pe", lambda e: e.matmul(yp[0:C, oc], lhsT=RT[pj, c0:c1], rhs=Sb[pj, :], start=True, stop=False), reads=[BRT, BS], writes=[Byp])
                                op("pe", lambda e: e.matmul(yp[0:C, oc], lhsT=v3(SCt[2])[:, q, :], rhs=TMv[:, ch, 2, pj], start=False, stop=False), reads=[BSC[2], BTM], writes=[Byp])
                                op("pe", lambda e: e.matmul(yp[0:C, oc], lhsT=v3(SCt[3])[:, q, :], rhs=UTt[0:C, oc], start=False, stop=True), reads=[BSC[3], BUT], writes=[Byp])
                        ch0 = units[0][0]
                        op("act", lambda e: e.activation(out=ytm[0:C, ci0 + ch0:ci0 + ch0 + nu, :], in_=yp[0:C, 0:nu * 128].rearrange("p (u f) -> p u f", f=128), func=AF.Copy),
                           reads=[Byp], writes=[Bytm])
                        sn, Bsn = pbank()
                        for u, (ch, Sf, Sb, BS, wc) in enumerate(units):
                            ocu = slice(u * 128, u * 128 + 128)
                            op("pe", lambda e: e.matmul(sn[:, ocu], lhsT=TMv[:, ch, 0, :], rhs=TMv[:, ch, 2, :], start=True, stop=False), reads=[BTM], writes=[Bsn])
                            op("pe", lambda e: e.matmul(sn[:, ocu], lhsT=TMv[:, ch, 1, :], rhs=UTt[0:C, ocu], start=False, stop=True), reads=[BTM, BUT], writes=[Bsn])
                        R["st_update"](units, sn, Bsn)
                        fill(); fill(); fill()
                        R["ck"](7.3)

                def st_update_prompt(units, sn, Bsn):
                    (ch, Sf, Sb, BS, wc) = units[0]
                    for j in range(2):
                        pj = slice(64 * j, 64 * j + 64)
                        op("dve", lambda e: e.tensor_tensor(out=Sf[pj, :], in0=sn[pj, pj], in1=Sf[pj, :], op=ALU.add), reads=[Bsn, BS], writes=[BS])
                    op("dve", lambda e: e.tensor_scalar(out=Sf, in0=Sf, scalar1=wc, scalar2=None, op0=ALU.mult), reads=[BS, Wsets[R["set"]]["ecp"][1]], writes=[BS])
                    op("act", lambda e: e.activation(out=Sb, in_=Sf, func=AF.Copy), reads=[BS], writes=[BS])

                def st_update_sample(units, sn, Bsn):
                    b0 = units[0][0] + R["b_base"]
                    nu = len(units)
                    Sf = STs[:, b0:b0 + nu, :]; Sb = STsb[:, b0:b0 + nu, :]
                    W = Wsets[R["set"]]
                    ecp = W["ecp"][0]
                    wcs = ecp[:, 0:NS].rearrange("p (b t) -> p b t", t=4)[:, b0:b0 + nu, 3]
                    for j in range(2):
                        pj = slice(64 * j, 64 * j + 64)
                        op("dve", lambda e: e.tensor_tensor(out=STs[pj, b0:b0 + nu, :], in0=sn[pj, 0:nu * 128].rearrange("p (u f) -> p u f", f=128)[:, :, pj], in1=STs[pj, b0:b0 + nu, :], op=ALU.add),
                           reads=[Bsn, BSTs], writes=[BSTs])
                    op("dve", lambda e: e.tensor_tensor(out=Sf, in0=Sf, in1=wcs.unsqueeze(2).broadcast_to([128, nu, 64]), op=ALU.mult), reads=[BSTs, W["ecp"][1]], writes=[BSTs])
                    op("act", lambda e: e.activation(out=Sb, in_=Sf, func=AF.Copy), reads=[BSTs], writes=[BSTs])

                def groupnorm(C, g0, ng_):
                    y4 = ytm[0:C, :, :].rearrange("p c (j v) -> p (c j) v", v=64)[:, g0:g0 + ng_, :]
                    gs = gst[0:C, 0:ng_, :]
                    op("dve", lambda e: e.tensor_reduce(out=gs[:, :, 0], in_=y4, axis=AX.X, op=ALU.add), reads=[Bytm], writes=[Bgst])
                    op("dve", lambda e: e.tensor_scalar(out=gs[:, :, 0], in0=gs[:, :, 0], scalar1=1.0 / 64, scalar2=None, op0=ALU.mult), reads=[Bgst], writes=[Bgst])
                    op("dve", lambda e: e.tensor_tensor(out=y4, in0=y4, in1=gs[:, :, 0:1].broadcast_to([C, ng_, 64]), op=ALU.subtract), reads=[Bytm, Bgst], writes=[Bytm])
                    sq_t = S_sq[0:C, 0:ng_ * 64].rearrange("p (g v) -> p g v", v=64)
                    op("pool", lambda e: e.tensor_tensor(out=sq_t, in0=y4, in1=y4, op=ALU.mult), reads=[Bytm], writes=[Bsqs])
                    op("dve", lambda e: e.tensor_reduce(out=gs[:, :, 1], in_=sq_t, axis=AX.X, op=ALU.add), reads=[Bsqs], writes=[Bgst])
                    op("act", lambda e: e.activation(out=gs[:, :, 1], in_=gs[:, :, 1], func=AF.Ln, bias=eps_t[0:C, 2:3], scale=1.0 / 64), reads=[Bgst, Bsm], writes=[Bgst])
                    op("act", lambda e: e.activation(out=gs[:, :, 1], in_=gs[:, :, 1], func=AF.Exp, scale=-0.5), reads=[Bgst], writes=[Bgst])
                    op("dve", lambda e: e.tensor_tensor(out=y4, in0=y4, in1=gs[:, :, 1:2].broadcast_to([C, ng_, 64]), op=ALU.mult), reads=[Bytm, Bgst], writes=[Bytm])

                def make_hp(hp):
                    wsl = []
                    rvv = lambda q, hp=hp: rv[:, q, hp:hp + 1]
                    hc = slice(hp * 128, hp * 128 + 128)

                    def load():
                        wsl.extend(load_w(wslice(w_in[l], 1024 + (q * 4 + hp) * 128), 8) for q in range(3))

                    def init():
                        op("dve", lambda e: e.memset(ST[:], 0.0), writes=[BST])
                        op("dve", lambda e: e.memset(STb[:], 0.0), writes=[BST])
                        S.dma("sp", STs[:], wkv_st[l, :, 2 * hp:2 * hp + 2].rearrange("b j k v -> (j k) b v"), "sts", writes=[BSTs])
                        op("act", lambda e: e.activation(out=STsb[:], in_=STs[:], func=AF.Copy), reads=[BSTs], writes=[BSTs])

                    def pre(ti, st, hp=hp, wsl=wsl, rvv=rvv, hc=hc):
                        c0, n = MT[ti]
                        smp = ti == SMP
                        W = Wsets[st]; WB = WBsets[st]; WF = WFsets[st]
                        for q, nm in enumerate(["r", "k", "v"]):
                            proj_shift(q * 4 + hp, wsl[q][0], wsl[q][1], ti, c0, n, W[nm][0], W[nm][1])
                            yield
                        r_, Br = W["r"]; k_, Bk = W["k"]; v_, Bv = W["v"]; lw, Blw = W["lw"]; cum, Bcum = W["cum"]
                        ecp, Becp = W["ecp"]; ecn, Becn = W["ecn"]; ar, Bar = W["ar"]; kk, Bkk = W["kk"]
                        g_, Bg = W["g"]; bon, Bbon = W["bon"]
                        RTb, BRT = WB["RT"]; KTb, BKT = WB["KT"]; ATb, BAT = WB["AT"]; BTb, BBT = WB["BT"]; Vb, BVb = WB["V"]
                        rmv = rmask[:, 256:320] if smp else rmask[:, 0:256]
                        pw, Bpw = pbank()
                        op("pe", lambda e: e.matmul(pw[:, 0:n], lhsT=w2a2b[0:64, hc], rhs=LR1[0:64, c0:c0 + n], start=True, stop=True), reads=[Blr, BLRs[0]], writes=[Bpw])
                        pa_, Bpa_ = pbank()
                        op("pe", lambda e: e.matmul(pa_[:, 0:n], lhsT=w2a2b[64:128, hc], rhs=LR1[64:128, c0:c0 + n], start=True, stop=True), reads=[Blr, BLRs[0]], writes=[Bpa_])
                        yield
                        pg2, Bpg2 = pbank()
                        op("pe", lambda e: e.matmul(pg2[:, 0:n], lhsT=g2b[:, hc], rhs=LR2[:, c0:c0 + n], start=True, stop=True), reads=[Blr, BLRs[1]], writes=[Bpg2])
                        op("act", lambda e: e.activation(out=lw[:, 0:n], in_=pw[:, 0:n], func=AF.Sigmoid, bias=rvv(0)), reads=[Bpw, Bpv], writes=[Blw])
                        op("act", lambda e: e.activation(out=ar[:, 0:n], in_=pa_[:, 0:n], func=AF.Sigmoid, bias=rvv(1)), reads=[Bpa_, Bpv], writes=[Bar])
                        yield
                        op("act", lambda e: e.activation(out=g_[:, 0:n], in_=pg2[:, 0:n], func=AF.Copy), reads=[Bpg2], writes=[Bg])
                        op("dve", lambda e: e.tensor_scalar(out=lw[:, 0:n], in0=lw[:, 0:n], scalar1=-0.6065306597126334, scalar2=None, op0=ALU.mult), reads=[Blw], writes=[Blw])
                        op("dve", lambda e: e.tensor_tensor_scan(out=cum[:, 0:n], data0=rmv, data1=lw[:, 0:n], initial=0.0, op0=ALU.mult, op1=ALU.add),
                           reads=[Brm, Blw], writes=[Bcum])
                        yield
                        op("dve", lambda e: e.tensor_tensor(out=lw[:, 0:n], in0=cum[:, 0:n], in1=lw[:, 0:n], op=ALU.subtract), reads=[Bcum, Blw], writes=[Blw])
                        t1, Bt1 = W["t1"]
                        op("dve", lambda e: e.tensor_scalar(out=kk[:, 0:n], in0=k_[:, 0:n], scalar1=rvv(2), scalar2=None, op0=ALU.mult), reads=[Bk, Bpv], writes=[Bkk])
                        op("pool", lambda e: e.tensor_tensor(out=t1[:, 0:n], in0=kk[:, 0:n], in1=kk[:, 0:n], op=ALU.mult), reads=[Bkk], writes=[Bt1])
                        yield
                        pss, Bpss = pbank()
                        op("pe", lambda e: e.matmul(pss[:, 0:n], lhsT=blk1, rhs=t1[:, 0:n], start=True, stop=True), reads=[Bc, Bt1], writes=[Bpss])
                        op("act", lambda e: e.activation(out=ecp[:, 0:n], in_=cum[:, 0:n], func=AF.Exp), reads=[Bcum], writes=[Becp])
                        op("act", lambda e: e.activation(out=ecn[:, 0:n], in_=cum[:, 0:n], func=AF.Exp, scale=-1.0), reads=[Bcum], writes=[Becn])
                        yield
                        op("act", lambda e: e.activation(out=lw[:, 0:n], in_=lw[:, 0:n], func=AF.Exp), reads=[Blw], writes=[Blw])
                        op("dve", lambda e: e.tensor_scalar(out=t1[:, 0:n], in0=pss[:, 0:n], scalar1=1e-18, scalar2=None, op0=ALU.max), reads=[Bpss], writes=[Bt1])
                        op("act", lambda e: e.activation(out=t1[:, 0:n], in_=t1[:, 0:n], func=AF.Ln), reads=[Bt1], writes=[Bt1])
                        yield
                        op("act", lambda e: e.activation(out=t1[:, 0:n], in_=t1[:, 0:n], func=AF.Exp, scale=-0.5), reads=[Bt1], writes=[Bt1])
                        op("dve", lambda e: e.tensor_tensor(out=kk[:, 0:n], in0=kk[:, 0:n], in1=t1[:, 0:n], op=ALU.mult), reads=[Bkk, Bt1], writes=[Bkk])
                        op("dve", lambda e: e.tensor_scalar(out=t1[:, 0:n], in0=ar[:, 0:n], scalar1=-1.0, scalar2=rvv(3), op0=ALU.add, op1=ALU.mult), reads=[Bar, Bpv], writes=[Bt1])
                        yield
                        op("dve", lambda e: e.scalar_tensor_tensor(out=k_[:, 0:n], in0=t1[:, 0:n], scalar=1.0, in1=k_[:, 0:n], op0=ALU.add, op1=ALU.mult), reads=[Bt1, Bk], writes=[Bk])
                        op("dve", lambda e: e.scalar_tensor_tensor(out=t1[:, 0:n], in0=r_[:, 0:n], scalar=rvv(4), in1=k_[:, 0:n], op0=ALU.mult, op1=ALU.mult), reads=[Br, Bk, Bpv], writes=[Bt1])
                        pbs, Bpbs = pbank()
                        op("pe", lambda e: e.matmul(pbs[:, 0:n], lhsT=blk1, rhs=t1[:, 0:n], start=True, stop=True), reads=[Bc, Bt1], writes=[Bpbs])
                        yield
                        op("dve", lambda e: e.tensor_tensor(out=bon[:, 0:n], in0=pbs[:, 0:n], in1=v_[:, 0:n], op=ALU.mult), reads=[Bpbs, Bv], writes=[Bbon])
                        op("pool", lambda e: e.tensor_tensor(out=RTb[:, 0:n], in0=r_[:, 0:n], in1=ecp[:, 0:n], op=ALU.mult), reads=[Br, Becp], writes=[BRT])
                        op("pool", lambda e: e.tensor_tensor(out=KTb[:, 0:n], in0=k_[:, 0:n], in1=ecn[:, 0:n], op=ALU.mult), reads=[Bk, Becn], writes=[BKT])
                        yield
                        op("pool", lambda e: e.tensor_copy(out=Vb[:, 0:n], in_=v_[:, 0:n]), reads=[Bv], writes=[BVb])
                        ATf, BATf = WF["ATf"]; BTf, BBTf = WF["BTf"]
                        op("dve", lambda e: e.tensor_tensor(out=t1[:, 0:n], in0=kk[:, 0:n], in1=ar[:, 0:n], op=ALU.mult), reads=[Bkk, Bar], writes=[Bt1])
                        op("dve", lambda e: e.tensor_tensor(out=BTf[:, 0:n], in0=t1[:, 0:n], in1=ecn[:, 0:n], op=ALU.mult), reads=[Bt1, Becn], writes=[BBTf])
                        yield
                        op("dve", lambda e: e.scalar_tensor_tensor(out=ATf[:, 0:n], in0=kk[:, 0:n], scalar=-1.0, in1=lw[:, 0:n], op0=ALU.mult, op1=ALU.mult), reads=[Bkk, Blw], writes=[BATf])
                        op("pool", lambda e: e.tensor_copy(out=BTb[:, 0:n], in_=BTf[:, 0:n]), reads=[BBTf], writes=[BBT])
                        op("pool", lambda e: e.tensor_copy(out=ATb[:, 0:n], in_=ATf[:, 0:n]), reads=[BATf], writes=[BAT])
                        yield

                    def core(ti, st, hp=hp, rvv=rvv):
                        c0, n = MT[ti]
                        smp = ti == SMP
                        W = Wsets[st]
                        R["set"] = st
                        ecp, Becp = W["ecp"]; g_, Bg = W["g"]; bon, Bbon = W["bon"]
                        yfm, Byfm = yfm_t
                        R["ck"](6.7)
                        if not smp:
                            nci, C = n // CH, CH
                            chunks = [(ci * CH, ci * CH + CH) for ci in range(nci)]
                            R["st_update"] = st_update_prompt
                            units_list = [[(ci, ST[:, :], STb[:, :], BST, ecp[:, ci * CH + CH - 1:ci * CH + CH])] for ci in range(nci)]
                            wkv_group(chunks, CH, units_list, 0, st)
                            if ti == LASTP:
                                S.dma("sp", wkv_out[l, 0, 2 * hp:2 * hp + 2].rearrange("j k v -> (j k) v"), ST[:], "o2", reads=[BST])
                            groupnorm(C, 0, nci * 2)
                        else:
                            nci, C = NB, 4
                            R["st_update"] = st_update_sample
                            for g4 in range(4):
                                chunks = [((g4 * 4 + b) * 4, (g4 * 4 + b) * 4 + 4) for b in range(4)]
                                R["b_base"] = g4 * 4
                                units = [(b, STs[:, g4 * 4 + b, :], STsb[:, g4 * 4 + b, :], BSTs, None) for b in range(4)]
                                wkv_group(chunks, 4, [units], g4 * 4, st)
                            S.dma("sp", wkv_out[l, 1:NG, 2 * hp:2 * hp + 2].rearrange("b j k v -> (j k) b v"), STs[:], "o2", reads=[BSTs])
                            for q4 in range(4):
                                groupnorm(C, q4 * 8, 8)
                        pyt, Bpyt = pbank()
                        for ci in range(nci):
                            op("pe", lambda e, ci=ci: e.transpose(pyt[:, ci * C:ci * C + C], ytm[0:C, ci, :], ident[0:C, 0:C]), reads=[Bytm, Bc], writes=[Bpyt])
                        op("act", lambda e: e.activation(out=yfm[:, 0:n], in_=pyt[:, 0:n], func=AF.Identity, scale=rvv(5), bias=rvv(6)), reads=[Bpyt, Bpv], writes=[Byfm])
                        op("dve", lambda e: e.tensor_tensor(out=yfm[:, 0:n], in0=yfm[:, 0:n], in1=bon[:, 0:n], op=ALU.add), reads=[Byfm, Bbon], writes=[Byfm])
                        op("dve", lambda e: e.tensor_tensor(out=ymix[:, 4 + hp, c0:c0 + n], in0=yfm[:, 0:n], in1=g_[:, 0:n], op=ALU.mult), reads=[Byfm, Bg], writes=[Bbig])

                    return load, init, pre, core

                def drain(g):
                    for _ in g:
                        pass
                stages = [(hp_, ti_) for hp_ in range(4) for ti_ in range(len(MT))]
                ctx = {0: make_hp(0)}
                ctx[0][0]()
                R["fill"] = lambda: None
                drain(ctx[0][2](0, 0))
                for k_, (hp_, ti_) in enumerate(stages):
                    if ti_ == 0:
                        ctx[hp_][1]()
                    gen = iter(())
                    if k_ + 1 < len(stages):
                        nh, nt_i = stages[k_ + 1]
                        if nt_i == 0:
                            ctx[nh] = make_hp(nh)
                            ctx[nh][0]()
                        gen = ctx[nh][2](nt_i, (k_ + 1) % 2)
                    R["fill"] = lambda gen=gen: next(gen, None)
                    ctx[hp_][3](ti_, k_ % 2)
                    drain(gen)
                    R["fill"] = lambda: None
                S.dma("sp", shift_out[l].rearrange("(c p) g -> p c g", p=128), shO[:], "o1", reads=[Bout])

        def out_proj(l):
            G2 = mod[l][:, 40:48, :]
            pend = [load_w(wslice(w_out[l], 0), 8)]
            for i in range(8):
                wo, Bwo = pend.pop(0)
                if i + 1 < 8:
                    pend.append(load_w(wslice(w_out[l], (i + 1) * 128), 8))
                for ti, (c0, n) in enumerate(TT):
                    pb, Bp = pbank()
                    for kc in range(8):
                        op("pe", lambda e, kc=kc, pb=pb, wo=wo, c0=c0, n=n: e.matmul(pb[:, 0:n], lhsT=wo[:, kc, :], rhs=ymix[:, kc, c0:c0 + n], start=(kc == 0), stop=(kc == 7)),
                           reads=[Bwo, Bbig], writes=[Bp])
                    resid_add(pb, Bp, i, c0, n, G2, Bmod[l][1])

        def open_x(src):
            R["x"] = S.sb([128, 8, NT]); R["ptmp"] = S.sb([128, 512]); R["Bpt"] = S.buf()
            for i in range(8):
                S.dma("sp", R["x"][:, i, :], src[i * 128:(i + 1) * 128, :], f"x{i}", writes=[Bx[i]])

        STOP = float(os.environ.get("MK_STOP", "99"))

        class Stop(Exception):
            pass

        def ck(k):
            if STOP <= k:
                raise Stop()
        R["ck"] = ck
        cur = {}

        def trunk():
          cur["sc"] = Scope(); cur["sc"].__enter__()
          open_x(xT)
          ck(1)
          for l in range(L):
            pf = ffn_prefetch(l, 0)
            norm_mod(mod[l][:, 8:16, :], mod[l][:, 0:8, :], Bmod[l][0], h_out_fn)
            if os.environ.get("MK_DUMPH") and STOP <= 2:
                for i in range(8):
                    op("dve", lambda e: e.tensor_copy(out=R["x"][:, i, :], in_=hT[:, i, :]), reads=[Bh, Bx[i]], writes=[Bx[i]])
            ck(2)
            if l == 0:
                ffn_with_stream(l, 0, mod[l][:, 16:24, :], pf, [(0, j) for j in range(24, 72)], [(0, 1), (0, 2)])
            else:
                ffn(l, 0, mod[l][:, 16:24, :], first=pf)
            ck(3)
            for i in range(8):
                S.dma("sp", xscr[i * 128:(i + 1) * 128, :], R["x"][:, i, :], f"x{i}", reads=[Bx[i]])
            norm_mod(mod[l][:, 32:40, :], mod[l][:, 24:32, :], Bmod[l][1], h_out_fn)
            cur["sc"].__exit__(None, None, None); cur["sc"] = None
            ck(4)
            with Scope():
                mixer(l)
            ck(8)
            cur["sc"] = Scope(); cur["sc"].__enter__()
            open_x(xscr)
            out_proj(l)
            ck(9)
            pf = ffn_prefetch(l, 1)
            norm_mod(mod[l][:, 56:64, :], mod[l][:, 48:56, :], Bmod[l][2], h_out_fn)
            if l == 0:
                ffn_with_stream(l, 1, mod[l][:, 64:72, :], pf, [(1, j) for j in range(72)], [(1, 0), (1, 1), (1, 2)])
            else:
                ffn(l, 1, mod[l][:, 64:72, :], first=pf)
            ck(10)
          yo = [S.sb([128, 8, 256])] * 2; Byo = [S.buf()] * 2
          yTv = yT.rearrange("(k p) t -> p k t", p=128)
          norm_mod(fA_t[:, :, :], fsh_t[:, :, :], Bsm,
                   lambda ti, c0, n: (yo[ti % 2][:, :, 0:n], Byo[ti % 2]),
                   after=lambda ti, c0, n: S.dma("sp", yTv[:, :, c0:c0 + n], yo[ti % 2][:, :, 0:n], "yo", reads=[Byo[ti % 2]]), nbuf=1)

        try:
            trunk()
        except Stop:
            if os.environ.get("MK_DUMPX") and cur.get("sc") is not None:
                for i in range(8):
                    S.dma("sp", yT[i * 128:(i + 1) * 128, :], R["x"][:, i, :], f"x{i}", reads=[Bx[i]])
        if cur.get("sc") is not None:
            cur["sc"].__exit__(None, None, None)
        S.barrier()
        S.emit()
    return nc


_CACHE = {}


def _consts():
    c = np.zeros((128, 9, 128), np.float32)
    c[:, 0, :] = np.eye(128)
    c[0:64, 1, 0:64] = 1.0
    c[64:128, 1, 64:128] = 1.0
    su = np.triu(np.ones((128, 128), np.float32), 1)
    iu = np.triu(np.ones((128, 128), np.float32), 0)
    c[:, 2, :] = su
    c[:, 3, :] = su
    c[:, 4, :] = iu
    c[:, 5, :] = iu
    c[:, 6, :] = su.T
    c[:, 7, :] = 1.0
    rm = np.ones((128, 320), np.float32)
    rm[:, 0:256:CH] = 0.0
    rm[:, 256::4] = 0.0
    return c, rm


def kernel(**inp):
    f = lambda a: np.ascontiguousarray(np.asarray(a, dtype=np.float32))
    I = {k: f(v) for k, v in inp.items()}
    if "nc" not in _CACHE:
        _CACHE["nc"] = build_program()
    nc = _CACHE["nc"]
    consts, rmask = _consts()
    chunk4 = lambda v: f(v.reshape(L, 4, 128).transpose(0, 2, 1))
    shared = {
        "w_ada": I["w_ada"],
        "b_adaT": f(I["b_ada"].reshape(L, 72, 128).transpose(0, 2, 1)),
        "normsT": f(np.stack([I["ffn1_norm"], I["mix_norm"], I["ffn2_norm"]], 1).reshape(L, 3, 8, 128).transpose(0, 1, 3, 2)),
        "fnormT": f(I["final_norm"].reshape(8, 128).T),
        "ffn1_w_gate": I["ffn1_w_gate"], "ffn2_w_gate": I["ffn2_w_gate"],
        "ffn1_w_up": I["ffn1_w_up"], "ffn2_w_up": I["ffn2_w_up"],
        "ffn1_w_down": I["ffn1_w_down"], "ffn2_w_down": I["ffn2_w_down"],
        "w_in": I["w_in"], "w_out": I["w_out"],
        "mu_T": f(I["rwkv_mu"].reshape(L, 14, 128).transpose(0, 2, 1)),
        "w2a2": f(np.concatenate([I["rwkv_w2"], I["rwkv_a2"]], axis=1)),
        "g2": I["rwkv_g2"],
        "consts": consts, "rmask": rmask,
    }
    bd = np.zeros((L, 2, 4, 128, 128), np.float32)
    for q, nm in enumerate(["lru_wx", "lru_wa"]):
        for c in range(4):
            for j in range(2):
                bd[:, q, c, 64 * j:64 * j + 64, 64 * j:64 * j + 64] = I[nm][:, 2 * c + j]
    shared["bd_gate"] = bd
    lv = np.zeros((L, 128, 8, 4), np.float32)
    for j in range(4):
        lv[:, :, j, :] = chunk4(I["lru_conv_w"][:, j])
    for j, nm in enumerate(["lru_conv_b", "lru_bx", "lru_ba", "lru_lambda"]):
        lv[:, :, 4 + j, :] = chunk4(I[nm])
    shared["lru_vec"] = lv
    rvv = np.zeros((L, 128, 7, 4), np.float32)
    for j, nm in enumerate(["rwkv_w0", "rwkv_a0", "rwkv_k_k", "rwkv_k_a", "rwkv_r_k", "rwkv_ln_w", "rwkv_ln_b"]):
        rvv[:, :, j, :] = chunk4(I[nm].reshape(L, 512))
    shared["rw_vec"] = rvv

    in_maps = []
    for c in range(8):
        sb = slice(c * NB, (c + 1) * NB)
        xs = I["x_sample"][sb].reshape(NS, D)
        m = dict(shared)
        m["xT"] = f(np.concatenate([I["x_prompt"][c], xs], 0).T)
        m["cT"] = f(np.concatenate([I["c_prompt"][c:c + 1], I["c_sample"][sb]], 0).T)
        m["conv_st"] = f(I["state_lru_conv"][:, sb].transpose(0, 3, 1, 2))
        m["h_st"] = f(I["state_lru_h"][:, sb].transpose(0, 2, 1))
        m["shift_st"] = f(I["state_rwkv_shift"][:, sb].transpose(0, 2, 1))
        m["wkv_st"] = f(I["state_rwkv_wkv"][:, sb].transpose(0, 1, 2, 4, 3))
        in_maps.append(m)
    res = run_bass_kernel_spmd(nc, in_maps, core_ids=list(range(8))).results

    B = 8
    y_p = np.zeros((B, NP_, D), np.float32); y_s = np.zeros((B * NB, 4, D), np.float32)
    p_conv = np.zeros((L, B, 3, 512), np.float32); s_conv = np.zeros((L, B * NB, 3, 512), np.float32)
    p_h = np.zeros((L, B, 512), np.float32); s_h = np.zeros((L, B * NB, 512), np.float32)
    p_sh = np.zeros((L, B, 1792), np.float32); s_sh = np.zeros((L, B * NB, 1792), np.float32)
    p_wkv = np.zeros((L, B, 8, 64, 64), np.float32); s_wkv = np.zeros((L, B * NB, 8, 64, 64), np.float32)
    for c in range(8):
        r = res[c]
        sb = slice(c * NB, (c + 1) * NB)
        yt = np.asarray(r["yT"]).T
        y_p[c] = yt[:NP_]
        y_s[sb] = yt[NP_:].reshape(NB, 4, D)
        co = np.asarray(r["conv_out"]).transpose(0, 2, 3, 1)
        p_conv[:, c] = co[:, 0]; s_conv[:, sb] = co[:, 1:]
        ho = np.asarray(r["h_out"]).transpose(0, 2, 1)
        p_h[:, c] = ho[:, 0]; s_h[:, sb] = ho[:, 1:]
        so = np.asarray(r["shift_out"]).transpose(0, 2, 1)
        p_sh[:, c] = so[:, 0]; s_sh[:, sb] = so[:, 1:]
        wo = np.asarray(r["wkv_out"]).transpose(0, 1, 2, 4, 3)
        p_wkv[:, c] = wo[:, 0]; s_wkv[:, sb] = wo[:, 1:]
    return (y_p, y_s, p_conv, p_h, p_sh, p_wkv, s_conv, s_h, s_sh, s_wkv)
```
